# Optimizing a Trainium2 kernel written in Bass

```python
import math
import jax, jax.numpy as jnp
from jax import lax
import numpy as np

D_MODEL = 2048
BATCH = 2
SEQ = 16384
DEPTH = 2

HEAD_DIM = 128
A_GROUPS = ((128, 1), (512, 4), (2048, 16))
A_HEADS_PER_GROUP = 4
A_HEADS = A_HEADS_PER_GROUP * len(A_GROUPS)
A_BLOCK = 64
A_QKV_W = A_HEADS * HEAD_DIM
A_OUT_W = A_HEADS_PER_GROUP * HEAD_DIM
B_Q_HEADS = 8
B_KV_HEADS = 2
B_HALF = 128
B_BLOCK = 128
B_Q_W = B_Q_HEADS * HEAD_DIM
B_KV_W = B_KV_HEADS * HEAD_DIM
NUM_BUCKETS = 32
MAX_DISTANCE = 1024
N_BIAS_HEADS = A_HEADS + B_Q_HEADS
SPLITS = (A_QKV_W, A_QKV_W, A_QKV_W, B_Q_W, B_KV_W, B_KV_W, D_MODEL, D_MODEL)
C_IN = sum(SPLITS)
D_FF = -(-8 * D_MODEL // (3 * 256)) * 256
DEEPNORM_ALPHA = (2 * DEPTH) ** 0.25
DEEPNORM_BETA = (8 * DEPTH) ** -0.25
LN_EPS = 1e-5
NEG_INF = -1e30

kernel_name = 'hybrid_dilated_window_gqa_encoder'


def t5_bucket(rel):
    half = NUM_BUCKETS // 2
    max_exact = half // 2
    n = np.abs(rel)
    scaled = np.log(np.maximum(n, 1) / max_exact) / math.log(MAX_DISTANCE / max_exact)
    large = np.minimum(max_exact + (scaled * (half - max_exact)).astype(np.int64), half - 1)
    return (np.where(rel > 0, half, 0) + np.where(n < max_exact, n, large)).astype(np.int32)


def band_offsets(blk):
    qi = np.arange(blk)[:, None]
    kj = np.arange(3 * blk)[None, :]
    return kj - blk - qi


def banded_attention(q, k, v, bias, half, blk, sink=None):
    n_, hk, r, seq_len, dh = q.shape
    nb = -(-seq_len // blk)
    lp = nb * blk
    qb = jnp.pad(q, ((0, 0), (0, 0), (0, 0), (0, lp - seq_len), (0, 0)))
    qb = qb.reshape(n_, hk, r, nb, blk, dh).astype(jnp.float32)

    def windows(t):
        tp = jnp.pad(t, ((0, 0), (0, 0), (blk, lp - seq_len + blk), (0, 0)))
        tp = tp.reshape(n_, hk, nb + 2, blk, dh)
        return jnp.concatenate([tp[:, :, :nb], tp[:, :, 1:nb + 1], tp[:, :, 2:]], axis=3).astype(jnp.float32)

    kw, vw = windows(k), windows(v)
    rel = band_offsets(blk)
    key_pos = (np.arange(nb)[:, None] - 1) * blk + np.arange(3 * blk)[None, :]
    mask = (np.abs(rel) <= half)[None] & ((key_pos >= 0) & (key_pos < seq_len))[:, None, :]
    s = jnp.einsum('nhrcid,nhcjd->nhrcij', qb, kw) * (dh ** -0.5) + bias[:, :, None]
    s = jnp.where(mask, s, NEG_INF)
    m = s.max(-1)
    if sink is not None:
        sk = sink.astype(jnp.float32)[:, :, None, None]
        m = jnp.maximum(m, sk)
    p = jnp.exp(s - m[..., None])
    den = p.sum(-1)
    if sink is not None:
        den = den + jnp.exp(sk - m)
    o = jnp.einsum('nhrcij,nhcjd->nhrcid', p, vw) / den[..., None]
    o = o.reshape(n_, hk, r, lp, dh)[..., :seq_len, :]
    lse = (m + jnp.log(den)).reshape(n_, hk, r, lp)[..., :seq_len]
    return o, lse


def dilated_mixture(q, k, v, rel_bias):
    bt, s_len, _ = q.shape
    hg = A_HEADS_PER_GROUP
    shp = (bt, s_len, len(A_GROUPS), hg, HEAD_DIM)
    q, k, v = q.reshape(shp), k.reshape(shp), v.reshape(shp)
    rel = band_offsets(A_BLOCK)
    outs, lses = [], []
    for g, (window, dil) in enumerate(A_GROUPS):
        half = window // (2 * dil)
        sub_len = s_len // dil

        def strided(t):
            t = t[:, :, g].reshape(bt, sub_len, dil, hg, HEAD_DIM)
            return t.transpose(0, 2, 3, 1, 4).reshape(bt * dil, hg, sub_len, HEAD_DIM)

        cols = slice(g * hg, (g + 1) * hg)
        bias = rel_bias[t5_bucket(rel * dil)][..., cols]
        bias = bias.transpose(2, 0, 1)[:, None].astype(jnp.float32)
        o, lse = banded_attention(strided(q)[:, :, None], strided(k), strided(v), bias, half, A_BLOCK)
        o = o[:, :, 0].reshape(bt, dil, hg, sub_len, HEAD_DIM).transpose(0, 3, 1, 2, 4)
        outs.append(o.reshape(bt, s_len, hg, HEAD_DIM))
        lse = lse[:, :, 0].reshape(bt, dil, hg, sub_len).transpose(0, 3, 1, 2)
        lses.append(lse.reshape(bt, s_len, hg))
    w = jax.nn.softmax(jnp.stack(lses), axis=0)
    o = jnp.einsum('gbsh,gbshd->bshd', w, jnp.stack(outs))
    return o.reshape(bt, s_len, A_OUT_W)


def windowed_gqa(q, k, v, rel_bias, sink):
    bt, s_len, _ = q.shape
    rep = B_Q_HEADS // B_KV_HEADS
    q = q.reshape(bt, s_len, B_KV_HEADS, rep, HEAD_DIM).transpose(0, 2, 3, 1, 4)
    k = k.reshape(bt, s_len, B_KV_HEADS, HEAD_DIM).transpose(0, 2, 1, 3)
    v = v.reshape(bt, s_len, B_KV_HEADS, HEAD_DIM).transpose(0, 2, 1, 3)
    bias = rel_bias[t5_bucket(band_offsets(B_BLOCK))][..., A_HEADS:]
    bias = bias.transpose(2, 0, 1).reshape(B_KV_HEADS, rep, B_BLOCK, 3 * B_BLOCK).astype(jnp.float32)
    o, _ = banded_attention(q, k, v, bias, B_HALF, B_BLOCK, sink.reshape(B_KV_HEADS, rep))
    return o.transpose(0, 3, 1, 2, 4).reshape(bt, s_len, B_Q_W)


def layer_norm(x, g, b):
    xf = x.astype(jnp.float32)
    mu = xf.mean(-1, keepdims=True)
    var = jnp.square(xf - mu).mean(-1, keepdims=True)
    return ((xf - mu) * lax.rsqrt(var + LN_EPS) * g + b).astype(x.dtype)


def hybrid_layer(x, rel_bias, w_in, sink, w_pa, w_pb, w_out, ln1_g, ln1_b, w_up, w_down, ln2_g, ln2_b):
    dt = x.dtype
    split_points = [int(c) for c in np.cumsum(SPLITS)[:-1]]
    proj = jnp.einsum('bsd,dc->bsc', x, w_in)
    qa, ka, va, qb, kb, vb, ga, gb = jnp.split(proj, split_points, axis=-1)
    ya = dilated_mixture(qa, ka, va, rel_bias).astype(dt) @ w_pa
    yb = windowed_gqa(qb, kb, vb, rel_bias, sink).astype(dt) @ w_pb
    merged = jax.nn.sigmoid(ga) * ya + jax.nn.sigmoid(gb) * yb
    x = layer_norm(DEEPNORM_ALPHA * x + merged @ w_out, ln1_g, ln1_b)
    gate, up = jnp.split(x @ w_up, 2, axis=-1)
    x = layer_norm(DEEPNORM_ALPHA * x + (jax.nn.silu(gate) * up) @ w_down, ln2_g, ln2_b)
    return x


def setup_inputs(seed: int = 0) -> dict:
    key = jax.random.key(seed)
    ks = jax.random.split(key, 13)
    beta = DEEPNORM_BETA
    nrm = jax.random.normal
    col_scale = np.ones((C_IN,), np.float32)
    col_scale[2 * A_QKV_W:3 * A_QKV_W] = beta
    vb0 = 3 * A_QKV_W + B_Q_W + B_KV_W
    col_scale[vb0:vb0 + B_KV_W] = beta
    x = nrm(ks[0], (BATCH, SEQ, D_MODEL), jnp.float32)
    rel_bias = 0.5 * nrm(ks[1], (NUM_BUCKETS, N_BIAS_HEADS), jnp.float32)
    w_in = nrm(ks[2], (DEPTH, D_MODEL, C_IN), jnp.float32) * (D_MODEL ** -0.5) * jnp.asarray(col_scale)
    sink = 0.5 * nrm(ks[3], (DEPTH, B_Q_HEADS), jnp.float32)
    w_pa = nrm(ks[4], (DEPTH, A_OUT_W, D_MODEL), jnp.float32) * (A_OUT_W ** -0.5) * beta
    w_pb = nrm(ks[5], (DEPTH, B_Q_W, D_MODEL), jnp.float32) * (B_Q_W ** -0.5) * beta
    w_out = nrm(ks[6], (DEPTH, D_MODEL, D_MODEL), jnp.float32) * (D_MODEL ** -0.5) * beta
    ln1_g = 1.0 + 0.02 * nrm(ks[7], (DEPTH, D_MODEL), jnp.float32)
    ln1_b = 0.02 * nrm(ks[8], (DEPTH, D_MODEL), jnp.float32)
    w_up = nrm(ks[9], (DEPTH, D_MODEL, 2 * D_FF), jnp.float32) * (D_MODEL ** -0.5) * beta
    w_down = nrm(ks[10], (DEPTH, D_FF, D_MODEL), jnp.float32) * (D_FF ** -0.5) * beta
    ln2_g = 1.0 + 0.02 * nrm(ks[11], (DEPTH, D_MODEL), jnp.float32)
    ln2_b = 0.02 * nrm(ks[12], (DEPTH, D_MODEL), jnp.float32)
    return {'x': x, 'rel_bias': rel_bias, 'w_in': w_in, 'sink': sink, 'w_pa': w_pa, 'w_pb': w_pb,
            'w_out': w_out, 'ln1_g': ln1_g, 'ln1_b': ln1_b, 'w_up': w_up, 'w_down': w_down,
            'ln2_g': ln2_g, 'ln2_b': ln2_b}


def reference(x, rel_bias, w_in, sink, w_pa, w_pb, w_out, ln1_g, ln1_b, w_up, w_down, ln2_g, ln2_b):
    for l in range(DEPTH):
        x = hybrid_layer(x, rel_bias, w_in[l], sink[l], w_pa[l], w_pb[l], w_out[l],
                         ln1_g[l], ln1_b[l], w_up[l], w_down[l], ln2_g[l], ln2_b[l])
    return x
```

```python
import contextlib
import math
import numpy as np
import ml_dtypes
import concourse.bass as bass
import concourse.mybir as mybir
from concourse.bass_utils import run_bass_kernel_spmd

F32 = mybir.dt.float32
BF16 = mybir.dt.bfloat16
AF = mybir.ActivationFunctionType
ALU = mybir.AluOpType

D = 2048
DFF = 5632
CIN = 10240
SEQ = 16384
EXT = 8192
T = 512
NT = EXT // T
VW = 1920
SCALE = 128.0 ** -0.5
ALPHA = 4.0 ** 0.25
LN_EPS = 1e-5
NEG = -30000.0
UNIT = 4096
NU1 = 41
NU2 = 84
NUNIT = NU1 + NU2
SQ = 2048


class Op:
    __slots__ = ("eng", "emit", "reads", "writes", "dma", "waits", "ms", "dsem", "dval")

    def __init__(self, eng, emit, reads, writes, dma):
        self.eng = eng
        self.emit = emit
        self.reads = reads
        self.writes = writes
        self.dma = dma
        self.waits = []
        self.ms = None
        self.dsem = None
        self.dval = None


class Sched:
    ENGS = ("pe", "act", "dve", "pool", "sp")
    SELF_SYNC = ("pool",)
    NDSEM = {"sp": 12, "pool": 24, "act": 4}

    def __init__(self, nc):
        self.nc = nc
        self.ops = {e: [] for e in self.ENGS}
        self.all = []
        self.lastw = {}
        self.readers = {}
        self.dcount = {}
        self.drr = {e: 0 for e in self.ENGS}
        self.dlast = {}

    def add(self, eng, emit, reads=(), writes=(), dma=False, ss=False):
        op = Op(eng, emit, tuple(reads), tuple(writes), dma)
        deps = []
        for r in op.reads:
            w = self.lastw.get(r)
            if w is not None:
                deps.append(w)
        for r in op.writes:
            w = self.lastw.get(r)
            if w is not None:
                deps.append(w)
            deps.extend(self.readers.get(r, ()))
        if dma:
            n = self.NDSEM[eng]
            slot = self.drr[eng] % n
            self.drr[eng] += 1
            key = (eng, slot)
            prev = self.dlast.get(key)
            if prev is not None:
                deps.append(prev)
            c = self.dcount.get(key, 0) + 1
            self.dcount[key] = c
            op.dsem = key
            op.dval = 16 * c
            self.dlast[key] = op
        seen = set()
        for d in deps:
            if id(d) in seen or d is op:
                continue
            seen.add(id(d))
            if d.dma:
                op.waits.append(("d", d.dsem, d.dval, d))
            else:
                if d.eng == eng and eng not in self.SELF_SYNC and not ss:
                    continue
                op.waits.append(("e", d.eng, None, d))
        for r in op.reads:
            self.readers.setdefault(r, []).append(op)
        for r in op.writes:
            self.lastw[r] = op
            self.readers[r] = []
        self.ops[eng].append(op)
        self.all.append(op)
        return op

    def barrier(self):
        lasts = []
        for e in self.ENGS:
            for op in reversed(self.ops[e]):
                if (not op.dma) and op.emit is not None:
                    lasts.append(op)
                    break
        dlasts = list(self.dlast.values())
        for e in self.ENGS:
            op = Op(e, None, (), (), False)
            for d in lasts:
                if d.eng != e:
                    op.waits.append(("e", d.eng, None, d))
            for d in dlasts:
                op.waits.append(("d", d.dsem, d.dval, d))
            self.ops[e].append(op)
            self.all.append(op)
        self.lastw = {}
        self.readers = {}

    def finalize(self):
        nc = self.nc
        for op in self.all:
            for w in op.waits:
                if w[0] == "e":
                    w[3].ms = -1
        cnt = {e: 0 for e in self.ENGS}
        for e in self.ENGS:
            for op in self.ops[e]:
                if op.ms == -1:
                    cnt[e] += 1
                    op.ms = cnt[e]
        st = contextlib.ExitStack()
        esem = {e: st.enter_context(nc.semaphore("s_" + e)) for e in self.ENGS}
        dsem = {}
        for key in self.dcount:
            dsem[key] = st.enter_context(nc.semaphore("d_%s%d" % key))
        block = st.enter_context(nc.Block())
        engobj = {"pe": "tensor", "act": "scalar", "dve": "vector", "pool": "gpsimd", "sp": "sync"}

        def run(e, eng):
            waited = {}
            for op in self.ops[e]:
                for w in op.waits:
                    if w[0] == "e":
                        sem = esem[w[1]]
                        val = w[3].ms
                        k = ("e", w[1])
                    else:
                        sem = dsem[w[1]]
                        val = w[2]
                        k = ("d", w[1])
                    if waited.get(k, 0) >= val:
                        continue
                    waited[k] = val
                    eng.wait_ge(sem, val)
                if op.emit is None:
                    continue
                ins = op.emit(eng)
                if op.dma:
                    ins.then_inc(dsem[op.dsem], 16)
                elif op.ms is not None:
                    ins.then_inc(esem[e], 1)

        for e in self.ENGS:
            if not self.ops[e]:
                continue
            getattr(block, engobj[e])(lambda eng, e=e: run(e, eng))
        st.close()


def t5_bucket(rel):
    half = 16
    max_exact = 8
    n = np.abs(rel)
    scaled = np.log(np.maximum(n, 1) / max_exact) / math.log(1024 / max_exact)
    large = np.minimum(max_exact + (scaled * (half - max_exact)).astype(np.int64), half - 1)
    return (np.where(rel > 0, half, 0) + np.where(n < max_exact, n, large)).astype(np.int32)


A_DIL = (1, 4, 16)


def bias_tiles(rel_bias):
    out = np.full((128, 48, 128), NEG, np.float32)
    p = np.arange(128)[:, None]
    f = np.arange(128)[None, :]
    for g in range(3):
        for h in range(4):
            head = g * 4 + h
            for j in range(2):
                rel = 128 * j - 64 + p - f
                vals = rel_bias[t5_bucket(rel * A_DIL[g]), head]
                out[:, head * 2 + j, :] = np.where(np.abs(rel) <= 64, vals, NEG)
    for h in range(8):
        for j in range(3):
            rel = 128 * j - 128 + p - f
            vals = rel_bias[t5_bucket(rel), 12 + h]
            out[:, 24 + h * 3 + j, :] = np.where(np.abs(rel) <= 128, vals, NEG)
    return out


K_CH = list(range(12, 24)) + [44, 45]
Q_CH = list(range(0, 12)) + list(range(36, 44))
G_CH = list(range(48, 80))


def unit_plan():
    units = []
    for i in range(0, 14, 2):
        units.append(("B", "w_in", 16, 0, [K_CH[i], K_CH[i + 1]]))
    for cb in range(4):
        c0 = 3072 + cb * 512 if cb < 3 else 5888
        nc_ = 512 if cb < 3 else 256
        for kg in range(2):
            units.append(("A", "w_in", kg * 8, 8, c0, nc_))
    for i in range(0, 20, 2):
        units.append(("B", "w_in", 16, 0, [Q_CH[i], Q_CH[i + 1]]))
    for i in range(0, 32, 2):
        units.append(("B", "w_in", 16, 0, [G_CH[i], G_CH[i + 1]]))
    assert len(units) == NU1
    for oc in range(0, 16, 2):
        units.append(("PAB", oc))
    for cb in range(4):
        for kg in range(2):
            units.append(("A", "w_out", kg * 8, 8, cb * 512, 512))
    for c in range(44):
        units.append(("B", "w_up", 16, 0, [c, 44 + c]))
    for cb in range(4):
        for kg in range(6):
            nk = 8 if kg < 5 else 4
            units.append(("A", "w_down", kg * 8, nk, cb * 512, 512))
    assert len(units) == NUNIT
    return units


UNITS = unit_plan()


def build_program(debug=False, nlayers=2):
    nc = bass.Bass("TRN2", target_bir_lowering=False)

    def din(name, shape, dt=F32):
        return nc.dram_tensor(name, list(shape), dt, kind="ExternalInput").ap()

    def dscr(name, shape, dt):
        kind = "ExternalOutput" if (debug and name != "WB") else "Internal"
        return nc.dram_tensor(name, list(shape), dt, kind=kind).ap()

    x_ext = din("x_ext", [EXT, D])
    validrep = din("validrep", [EXT, 128], BF16)
    validcol_d = din("validcol", [128, EXT // 128])
    bias_d = din("bias_all", [128, 48 * 128])
    ident_d = din("ident", [128, 128], BF16)
    sink_d = din("sink", [1, 16])
    ln_d = {n: din(n, [2, D]) for n in ("ln1_g", "ln1_b", "ln2_g", "ln2_b")}
    W = {
        "w_in": din("w_in", [2, D, CIN]),
        "w_pa": din("w_pa", [2, 512, D]),
        "w_pb": din("w_pb", [2, 1024, D]),
        "w_out": din("w_out", [2, D, D]),
        "w_up": din("w_up", [2, D, 2 * DFF]),
        "w_down": din("w_down", [2, DFF, D]),
    }
    out = nc.dram_tensor("out", [4096, D], F32, kind="ExternalOutput").ap()

    WB = dscr("WB", [2, NUNIT, 128, UNIT], BF16)
    QT = dscr("QT", [20, 128, EXT], BF16)
    KT = dscr("KT", [14, 128, EXT], BF16)
    VS = dscr("VS", [EXT, VW], BF16)
    GT = dscr("GT", [32, 128, EXT], BF16)
    OT = dscr("OT", [12, 128, EXT], BF16)
    X1 = dscr("X1", [EXT, D], F32)

    S = Sched(nc)
    ARENA_BYTES = 204 * 1024
    arena = nc.alloc_sbuf_tensor("arena", [128, ARENA_BYTES // 2], BF16)[:]
    psum = nc.alloc_psum_tensor("psum", [128, 8, 512], F32)[:]
    psum_bf = psum.bitcast(BF16)

    class Carver:
        def __init__(self, base):
            self.off = base

        def take(self, shape, dt):
            n = 1
            for s_ in shape[1:]:
                n *= s_
            nb = n * (4 if dt == F32 else 2)
            nb_al = (nb + 63) // 64 * 64
            v = arena[:, self.off // 2:(self.off + nb) // 2]
            if dt == F32:
                v = v.bitcast(F32)
            self.off += nb_al
            assert self.off <= ARENA_BYTES, self.off
            if len(shape) == 3:
                v = v.rearrange("p (a b) -> p a b", a=shape[1])
            elif len(shape) == 4:
                v = v.rearrange("p (a b c) -> p a b c", a=shape[1], b=shape[2])
            return v

    pc = Carver(0)
    ident = pc.take([128, 128], BF16)
    validcol = pc.take([128, EXT // 128], F32)
    esink = pc.take([128, 16], F32)
    stat = pc.take([128, 160], F32)
    PBASE = pc.off

    S.add("pool", lambda e: e.dma_start(out=ident, in_=ident_d), writes=["ident"], dma=True)
    S.add("pool", lambda e: e.dma_start(out=validcol, in_=validcol_d), writes=["validcol"], dma=True)
    S.add("pool", lambda e: e.dma_start(out=esink, in_=sink_d.to_broadcast([128, 16])), writes=["esink"], dma=True)
    S.add("act", lambda e: e.activation(esink, esink, AF.Exp), reads=["esink"], writes=["esink"])
    for i in range(8):
        S.add("pool", lambda e, i=i: e.dma_start(out=VS[i * 1024:(i + 1) * 1024, 1792:1920],
                                                 in_=validrep[i * 1024:(i + 1) * 1024, :]),
              writes=[("VSvalid", i)], dma=True)

    def conv_ops(L):
        ops = []
        for u, un in enumerate(UNITS):
            dst = WB[L, u]
            if un[0] == "B":
                _, wn, nk, _, chs = un
                for j, c in enumerate(chs):
                    src = W[wn][L][:, c * 128:(c + 1) * 128].rearrange("(kc p) j -> p kc j", p=128)
                    d = dst[:, j * nk * 128:(j + 1) * nk * 128].rearrange("p (kc j) -> p kc j", j=128)
                    ops.append((u, d, src))
            elif un[0] == "A":
                _, wn, k0, nk, c0, ncol = un
                src = W[wn][L][k0 * 128:(k0 + nk) * 128, c0:c0 + ncol].rearrange("(kc p) j -> p kc j", p=128)
                d = dst[:, 0:nk * ncol].rearrange("p (kc j) -> p kc j", j=ncol)
                ops.append((u, d, src))
            else:
                oc0 = un[1]
                for j in range(2):
                    oc = oc0 + j
                    base = j * 1536
                    src = W["w_pa"][L][:, oc * 128:(oc + 1) * 128].rearrange("(kc p) j -> p kc j", p=128)
                    d = dst[:, base:base + 512].rearrange("p (kc j) -> p kc j", j=128)
                    ops.append((u, d, src))
                    src = W["w_pb"][L][:, oc * 128:(oc + 1) * 128].rearrange("(kc p) j -> p kc j", p=128)
                    d = dst[:, base + 512:base + 1536].rearrange("p (kc j) -> p kc j", j=128)
                    ops.append((u, d, src))
        return ops

    conv_pending = {0: conv_ops(0), 1: conv_ops(1)}
    conv_done_parts = {}

    def pump_conv(L, n):
        lst = conv_pending[L]
        for _ in range(min(n, len(lst))):
            u, d, src = lst.pop(0)
            k = conv_done_parts.get((L, u), 0)
            conv_done_parts[(L, u)] = k + 1
            S.add("pool", lambda e, d=d, src=src: e.dma_start(out=d, in_=src),
                  writes=[("wb", L, u, k)], dma=True)

    def wb_res(L, u):
        return [("wb", L, u, k) for k in range(conv_done_parts.get((L, u), 0))]

    pump_conv(0, 74)

    rr = {"ev": 0}

    def evac(dst, src, reads, writes, func=None, eng=None):
        if eng is None:
            if func is not None:
                eng = "act"
            else:
                eng = ("dve", "act")[rr["ev"] % 2]
                rr["ev"] += 1
        if eng == "act":
            f = func if func is not None else AF.Copy
            S.add("act", lambda e: e.activation(dst, src, f), reads=reads, writes=writes)
        else:
            S.add("dve", lambda e: e.tensor_copy(dst, src), reads=reads, writes=writes)

    class WRing:
        def __init__(self, view, nslots):
            self.view = view
            self.n = nslots
            self.i = 0

        def load(self, L, u, nelem=UNIT):
            slot = self.i % self.n
            self.i += 1
            if L == 0 and u >= NU1 and self.i % 4 == 0:
                pump_conv(1, 1)
            dstv = self.view[:, slot, 0:nelem]
            srcv = WB[L, u][:, 0:nelem]
            S.add("sp", lambda e: e.dma_start(out=dstv, in_=srcv),
                  reads=wb_res(L, u), writes=[("wr", slot)], dma=True)
            return slot

    def load_x_tile(L, t, xin, key):
        src = (x_ext if L == 0 else X1)[t * T:(t + 1) * T, :].rearrange("(s p) k -> p s k", p=128)
        rd = [] if L == 0 else [("X1", t)]
        wk_ = [key] + ([("xt", s_) for s_ in range(4)] if key == "xt_all" else [])
        S.add("pool", lambda e: e.dma_start(out=xin, in_=src), reads=rd, writes=wk_, dma=True)

    def transpose_tile(src_bf, src_keys, dstT, dst_key, banks=(0, 1)):
        n = 0
        for s in range(4):
            for g in range(4):
                bank = banks[n % len(banks)]
                n += 1

                def tr(e, s=s, g=g, bank=bank):
                    ins = None
                    for j in range(4):
                        kc = g * 4 + j
                        ins = e.transpose(psum_bf[:, bank, j * 128:(j + 1) * 128],
                                          src_bf[:, s, kc * 128:(kc + 1) * 128], ident)
                    return ins
                S.add("pe", tr, reads=[src_keys[s], "ident"], writes=[("ps", bank)])
                srcv = psum_bf[:, bank, 0:512].rearrange("p (j t) -> p j t", j=4)
                dstv = dstT[:, g * 4:(g + 1) * 4, s * 128:(s + 1) * 128]
                evac(dstv, srcv, [("ps", bank)], [(dst_key, s, g)])
        return [(dst_key, s, g) for s in range(4) for g in range(4)]

    st6a = stat[:, 0:96].rearrange("p (s c) -> p s c", s=4)
    mva = stat[:, 96:104].rearrange("p (s c) -> p s c", s=4)
    rstda = stat[:, 104:108]

    def ln_stats_chunk(xt, xkey, s, cb):
        xs = xt[:, s, cb * 512:(cb + 1) * 512]
        S.add("dve", lambda e: e.bn_stats(st6a[:, s, cb * 6:(cb + 1) * 6], xs), reads=[(xkey, s)],
              writes=[("st6", s, cb)], ss=True)

    def layer_norm_all(xt, xkey, lng, lnb, lnkey, xbf_out, xbf_key, out_stage=None, stage_keys=()):
        for s in range(4):
            S.add("dve", lambda e, s=s: e.bn_aggr(mva[:, s, :], st6a[:, s, :]),
                  reads=[("st6", s, cb) for cb in range(4)], writes=[("mv", s)], ss=True)
        S.add("dve", lambda e: e.tensor_scalar(rstda, mva[:, :, 1], LN_EPS, None, ALU.add),
              reads=[("mv", s) for s in range(4)], writes=["rstd"], ss=True)
        S.add("act", lambda e: e.sqrt(rstda, rstda), reads=["rstd"], writes=["rstd"])
        S.add("dve", lambda e: e.reciprocal(rstda, rstda), reads=["rstd"], writes=["rstd"], ss=True)
        for s in range(4):
            xs = xt[:, s, :]
            S.add("dve", lambda e, xs=xs, s=s: e.tensor_scalar(xs, xs, mva[:, s, 0:1], rstda[:, s:s + 1], ALU.subtract, ALU.mult),
                  reads=[(xkey, s), "rstd", ("mv", s)], writes=[(xkey, s)], ss=True)
            S.add("dve", lambda e, xs=xs: e.tensor_tensor(xs, xs, lng, ALU.mult), reads=[(xkey, s), lnkey], writes=[(xkey, s)])
            if out_stage is None:
                S.add("pool", lambda e, xs=xs: e.tensor_tensor(xs, xs, lnb, ALU.add), reads=[(xkey, s), lnkey], writes=[(xkey, s)])
            else:
                ov = out_stage[:, s, :]
                S.add("pool", lambda e, xs=xs, ov=ov: e.tensor_tensor(ov, xs, lnb, ALU.add), reads=[(xkey, s), lnkey],
                      writes=[("stg", s)] + list(stage_keys))
            if xbf_out is not None:
                S.add("act", lambda e, xs=xs, s=s: e.activation(xbf_out[:, s, :], xs, AF.Copy), reads=[(xkey, s)],
                      writes=[(xbf_key, s)] + [("mT", oc) for oc in range(16)])

    for L in range(nlayers):
        if L == 1:
            pump_conv(1, 10 ** 6)
        kv_tiles = list(range(0, 16)) if L == 0 else list(range(2, 14))
        full_tiles = list(range(2, 14)) if L == 0 else list(range(4, 12))

        S.barrier()
        c1 = Carver(PBASE)
        xin = [c1.take([128, 4, D], F32) for _ in range(2)]
        xbf = c1.take([128, 4, D], BF16)
        xT = c1.take([128, 16, T], BF16)
        NW1 = 8
        wr1 = WRing(c1.take([128, NW1, UNIT], BF16), NW1)
        stB = c1.take([128, 4, 2, T], BF16)
        vst = c1.take([128, 4, 1792], BF16)
        stb_i = [0]

        load_x_tile(L, kv_tiles[0], xin[0], ("xin", 0))
        for ti, t in enumerate(kv_tiles):
            full = t in full_tiles
            xi = xin[ti % 2]
            xkey = ("xin", ti % 2)
            if ti + 1 < len(kv_tiles):
                load_x_tile(L, kv_tiles[ti + 1], xin[(ti + 1) % 2], ("xin", (ti + 1) % 2))
            for s in range(4):
                evac(xbf[:, s, :], xi[:, s, :], [xkey], [("xbf", s)])
            xT_keys = transpose_tile(xbf, [("xbf", s) for s in range(4)], xT, "xT")

            def b_units(u0, nun, dests, func):
                for k in range(nun):
                    slot = wr1.load(L, u0 + k)
                    sb = stb_i[0] % 4
                    stb_i[0] += 1
                    for j in range(2):
                        bank = 2 + j

                        def mm(e, slot=slot, j=j, bank=bank):
                            ins = None
                            wv = wr1.view[:, slot, j * 2048:(j + 1) * 2048].rearrange("p (kc c) -> p kc c", c=128)
                            for kc in range(16):
                                ins = e.matmul(psum[:, bank, :], wv[:, kc, :], xT[:, kc, :],
                                               start=(kc == 0), stop=(kc == 15))
                            return ins
                        S.add("pe", mm, reads=[("wr", slot)] + xT_keys, writes=[("ps", bank)])
                        evac(stB[:, sb, j, :], psum[:, bank, :], [("ps", bank)], [("stB", sb, j)], func=func)
                    (dt0, i0), (dt1, i1) = dests[2 * k], dests[2 * k + 1]
                    if dt0 is dt1 and i1 == i0 + 1:
                        dv = dt0[1][i0:i0 + 2, :, t * T:(t + 1) * T].rearrange("c p t -> p c t")
                        S.add("pool", lambda e, dv=dv, sb=sb: e.dma_start(out=dv, in_=stB[:, sb, :, :]),
                              reads=[("stB", sb, 0), ("stB", sb, 1)], writes=[(dt0[0], i0, t), (dt0[0], i1, t)], dma=True)
                    else:
                        for j, (dtj, ij) in enumerate(((dt0, i0), (dt1, i1))):
                            dv = dtj[1][ij, :, t * T:(t + 1) * T]
                            S.add("pool", lambda e, dv=dv, sb=sb, j=j: e.dma_start(out=dv, in_=stB[:, sb, j, :]),
                                  reads=[("stB", sb, j)], writes=[(dtj[0], ij, t)], dma=True)

            KTd = ("KT", KT)
            QTd = ("QT", QT)
            GTd = ("GT", GT)
            b_units(0, 7, [(KTd, i) for i in range(14)], None)
            u = 7
            for cb in range(4):
                ncol = 512 if cb < 3 else 256
                c0 = cb * 512
                slots = [wr1.load(L, u + kg, 8 * ncol) for kg in range(2)]
                u += 2
                for kg in range(2):
                    for s in range(4):
                        def mm(e, slot=slots[kg], kg=kg, s=s, ncol=ncol):
                            ins = None
                            wv = wr1.view[:, slot, 0:8 * ncol].rearrange("p (kc c) -> p kc c", c=ncol)
                            for k8 in range(8):
                                kc = kg * 8 + k8
                                ins = e.matmul(psum[:, 4 + s, 0:ncol], xT[:, kc, s * 128:(s + 1) * 128], wv[:, k8, :],
                                               start=(kc == 0), stop=(kc == 15))
                            return ins
                        S.add("pe", mm, reads=[("wr", slots[kg])] + xT_keys, writes=[("ps", 4 + s)])
                for s in range(4):
                    blk = t * 4 + s
                    dstv = vst[:, s, c0:c0 + ncol]
                    srcv = psum[:, 4 + s, 0:ncol]
                    S.add("dve", lambda e, dstv=dstv, srcv=srcv, blk=blk: e.tensor_scalar(
                        dstv, srcv, validcol[:, blk:blk + 1], None, ALU.mult),
                        reads=[("ps", 4 + s), "validcol"], writes=[("vst", s, cb)])
            for s in range(4):
                dv = VS[t * T + s * 128:t * T + (s + 1) * 128, 0:1792]
                S.add("pool", lambda e, dv=dv, s=s: e.dma_start(out=dv, in_=vst[:, s, :]),
                      reads=[("vst", s, cb) for cb in range(4)], writes=[("VS", t * 4 + s)], dma=True)
            if full:
                b_units(15, 10, [(QTd, i) for i in range(20)], None)
                b_units(25, 16, [(GTd, i) for i in range(32)], AF.Sigmoid)

        S.barrier()
        ca = Carver(PBASE)
        biasT = ca.take([128, 48, 128], F32)
        acc = ca.take([128, 2, 4 * SQ], F32).rearrange("p a (h t) -> p a h t", h=4)
        accn = acc[:, 0]
        accd = acc[:, 1]
        qg = ca.take([128, 4, SQ], BF16)
        kg_ = ca.take([128, 4, SQ + 2048], BF16)
        vg = ca.take([128, 18, 640], BF16)
        tS = [ca.take([128, 1024], F32) for _ in range(2)]
        pT = [ca.take([128, 1024], BF16) for _ in range(2)]
        ost = ca.take([128, 4, SQ], BF16)
        rden = [ca.take([128, 512], F32) for _ in range(2)]
        S.add("pool", lambda e: e.dma_start(out=biasT.rearrange("p a b -> p (a b)"), in_=bias_d), writes=["biasT"], dma=True)
        bcnt = [0]
        pend = [None]
        vhalf = [0]

        def kv_reads(lo, hi, heads, kind):
            t0, t1 = lo // T, (hi - 1) // T
            return [(kind, h, tt) for h in heads for tt in range(t0, t1 + 1)]

        def vs_reads(lo, hi):
            return [("VS", b) for b in range(lo // 128, (hi - 1) // 128 + 1)] + [("VSvalid", i) for i in range(8)]

        def run_batch(qblocks, nj, bias_idx0, vcol0, kind):
            b = bcnt[0] % 2
            bcnt[0] += 1
            if L == 0 and bcnt[0] % 2 == 0:
                pump_conv(0, 1)
            nq = qblocks[0]["nq"]
            nb = len(qblocks)
            sb0 = 2 * b

            def qk(e):
                ins = None
                for i, qb in enumerate(qblocks):
                    for j in range(nj):
                        col = (i * nj + j) * 128
                        ins = e.matmul(psum[:, sb0 + col // 512, col % 512:col % 512 + nq], qb["keys"][j], qb["q"],
                                       start=True, stop=True)
                return ins
            S.add("pe", qk, reads=qblocks[0]["reads"], writes=[("ps", sb0), ("ps", sb0 + 1)])
            ntot = nb * nj
            pv_s = psum[:, sb0:sb0 + 2, :].rearrange("p a b -> p (a b)")[:, 0:ntot * 128].rearrange(
                "p (i j c) -> p i j c", i=nb, j=nj)[:, :, :, 0:nq]
            tv = tS[b][:, 0:ntot * 128].rearrange("p (i j c) -> p i j c", i=nb, j=nj)[:, :, :, 0:nq]
            bv = biasT[:, bias_idx0:bias_idx0 + nj, 0:nq].unsqueeze(1).to_broadcast([128, nb, nj, nq])
            S.add("dve", lambda e: e.scalar_tensor_tensor(tv, pv_s, SCALE, bv, ALU.mult, ALU.add),
                  reads=[("ps", sb0), ("ps", sb0 + 1), "biasT"], writes=[("tS", b)])
            pv = pT[b][:, 0:ntot * 128].rearrange("p (i j c) -> p i j c", i=nb, j=nj)[:, :, :, 0:nq]
            S.add("act", lambda e: e.activation(pv, tv, AF.Exp), reads=[("tS", b)], writes=[("pT", b)])

            def pvm(e):
                ins = None
                for i, qb in enumerate(qblocks):
                    for j in range(nj):
                        col = (i * nj + j) * 128
                        ins = e.matmul(psum[:, 4 + b, i * 128:i * 128 + nq],
                                       vg[:, qb["vkb"][j], vcol0:vcol0 + 128], pT[b][:, col:col + nq],
                                       start=(j == 0), stop=(j == nj - 1))
                for i, qb in enumerate(qblocks):
                    for j in range(nj):
                        col = (i * nj + j) * 128
                        ins = e.matmul(psum[:, 6 + b, i * 128:i * 128 + nq],
                                       vg[:, qb["vkb"][j], 512:640], pT[b][:, col:col + nq],
                                       start=(j == 0), stop=(j == nj - 1))
                return ins
            po = psum[:, 4 + b, :].rearrange("p (i c) -> p i c", c=128)[:, 0:nb, 0:nq]
            pd = psum[:, 6 + b, :].rearrange("p (i c) -> p i c", c=128)[:, 0:nb, 0:nq]
            Lc = L

            def stageB():
                S.add("pe", pvm, reads=[("pT", b)] + qblocks[0].get("vkeys", [("vg", 0), ("vg", 1)]), writes=[("ps", 4 + b), ("ps", 6 + b)])
                if kind == "A":
                    av, akey = qblocks[0]["accv"]
                    pod = psum[:, 4 + b:8:2, :].rearrange("p a (i c) -> p a i c", c=128)[:, :, 0:nb, 0:nq]
                    S.add("dve", lambda e: e.tensor_tensor(av, av, pod, ALU.add),
                          reads=[("ps", 4 + b), ("ps", 6 + b), akey], writes=[akey])
                else:
                    hq, ov, okey = qblocks[0]["outv"]
                    rv = rden[b][:, 0:nb * 128].rearrange("p (i c) -> p i c", c=128)[:, :, 0:nq]
                    S.add("act", lambda e: e.activation(rv, pd, AF.Ln, bias=esink[:, Lc * 8 + hq:Lc * 8 + hq + 1], scale=1.0),
                          reads=[("ps", 6 + b), "esink"], writes=[("rden", b)])
                    S.add("act", lambda e: e.activation(rv, rv, AF.Exp, scale=-1.0), reads=[("rden", b)], writes=[("rden", b)], ss=True)
                    S.add("dve", lambda e: e.tensor_tensor(ov, po, rv, ALU.mult), reads=[("ps", 4 + b), ("rden", b)], writes=[okey])
            prev = pend[0]
            pend[0] = stageB
            if prev is not None:
                prev()

        def flush():
            f = pend[0]
            pend[0] = None
            if f is not None:
                f()

        q_lo = full_tiles[0] * T
        q_hi = (full_tiles[-1] + 1) * T
        for Q0 in range(q_lo, q_hi, SQ):
            S.add("pool", lambda e: e.memset(accn, 0.0), writes=["acc"])
            S.add("pool", lambda e: e.memset(accd, 1e-30), writes=["acc"])
            for g in range(3):
                dil = A_DIL[g]
                Wd = 64 * dil
                heads = [g * 4 + h for h in range(4)]
                flush()
                qsrc = QT[g * 4:(g + 1) * 4, :, Q0:Q0 + SQ].rearrange("h p t -> p h t")
                S.add("pool", lambda e, qsrc=qsrc: e.dma_start(out=qg, in_=qsrc),
                      reads=kv_reads(Q0, Q0 + SQ, heads, "QT"), writes=["qg"], dma=True)
                klen = SQ + 2 * Wd
                ksrc = KT[g * 4:(g + 1) * 4, :, Q0 - Wd:Q0 + SQ + Wd].rearrange("h p t -> p h t")
                S.add("pool", lambda e, ksrc=ksrc, klen=klen: e.dma_start(out=kg_[:, :, 0:klen], in_=ksrc),
                      reads=kv_reads(Q0 - Wd, Q0 + SQ + Wd, heads, "KT"), writes=["kg"], dma=True)
                nsub = SQ // dil
                nq = min(128, nsub)
                nqb = nsub // nq
                nkb = nqb + 1
                for r0 in range(0, dil, max(1, 4 // nqb) if nqb < 4 else 1):
                    rs = list(range(r0, min(dil, r0 + (max(1, 4 // nqb) if nqb < 4 else 1))))
                    if dil == 1:
                        flush()
                        vbase = 0
                        vkeys = [("vg", 0), ("vg", 1)]
                    else:
                        vhalf[0] ^= 1
                        vbase = 9 * vhalf[0]
                        vkeys = [("vg", vhalf[0])]
                    for ri, r in enumerate(rs):
                        for part, (c0, c1, d0) in enumerate(((g * 512, g * 512 + 512, 0), (1792, 1920, 512))):
                            row0 = Q0 + r - Wd
                            src = bass.AP(VS.tensor, row0 * VW + c0, [[dil * VW, 128], [128 * dil * VW, nkb], [1, c1 - c0]])
                            dstv = vg[:, vbase + ri * nkb:vbase + (ri + 1) * nkb, d0:d0 + (c1 - c0)]
                            S.add("pool", lambda e, src=src, dstv=dstv: e.dma_start(out=dstv, in_=src),
                                  reads=vs_reads(row0, row0 + nkb * 128 * dil), writes=vkeys, dma=True)
                    for h in range(4):
                        head = g * 4 + h
                        allqb = []
                        for ri, r in enumerate(rs):
                            for n in range(nqb):
                                i0 = n * nq
                                qv = qg[:, h, :].rearrange("p (i q) -> p i q", q=dil)[:, i0:i0 + nq, r]
                                keys = []
                                for j in range(2):
                                    ks = i0 + 128 * j
                                    kv_ = kg_[:, h, 0:klen].rearrange("p (i q) -> p i q", q=dil)[:, ks:ks + 128, r]
                                    keys.append(kv_)
                                an = accn[:, h, :].rearrange("p (i q) -> p i q", q=dil)
                                ad = accd[:, h, :].rearrange("p (i q) -> p i q", q=dil)
                                allqb.append(dict(q=qv, nq=nq, keys=keys, vkb=[vbase + ri * nkb + n, vbase + ri * nkb + n + 1],
                                                  r=r, i0=i0, reads=["qg", "kg"], vkeys=vkeys))
                        for b0 in range(0, len(allqb), 4):
                            grp = allqb[b0:b0 + 4]
                            a2 = acc[:, :, h, :].rearrange("p a (i q) -> p a i q", q=dil)
                            if len(rs) == 1:
                                r = rs[0]
                                i0 = grp[0]["i0"]
                                av = a2[:, :, i0:i0 + len(grp) * nq, r].rearrange("p a (n c) -> p a n c", c=nq)
                            else:
                                ra = grp[0]["r"]
                                av = a2[:, :, 0:nq, ra:ra + len(grp)].rearrange("p a c n -> p a n c")
                            grp[0]["accv"] = (av, "acc")
                            run_batch(grp, 2, head * 2, h * 128, "A")
            flush()
            S.add("act", lambda e: e.activation(accd, accd, AF.Ln), reads=["acc"], writes=["acc"])
            S.add("act", lambda e: e.activation(accd, accd, AF.Exp, scale=-1.0), reads=["acc"], writes=["acc"], ss=True)
            S.add("dve", lambda e: e.tensor_tensor(ost, accn, accd, ALU.mult), reads=["acc"], writes=["ost"])
            od = OT[0:4, :, Q0:Q0 + SQ].rearrange("h p t -> p h t")
            S.add("pool", lambda e, od=od: e.dma_start(out=od, in_=ost), reads=["ost"],
                  writes=[("OT", c, tt) for c in range(4) for tt in range(Q0 // T, (Q0 + SQ) // T)], dma=True)
            ksrc = KT[12:14, :, Q0 - 128:Q0 + SQ + 128].rearrange("h p t -> p h t")
            S.add("pool", lambda e, ksrc=ksrc: e.dma_start(out=kg_[:, 0:2, 0:SQ + 256], in_=ksrc),
                  reads=kv_reads(Q0 - 128, Q0 + SQ + 128, [12, 13], "KT"), writes=["kg"], dma=True)
            nkbB = SQ // 128 + 2
            flush()
            for part, (c0, c1, d0) in enumerate(((1536, 1792, 0), (1792, 1920, 512))):
                row0 = Q0 - 128
                src = VS[row0:row0 + nkbB * 128, c0:c1].rearrange("(kb p) c -> p kb c", p=128)
                dstv = vg[:, 0:nkbB, d0:d0 + (c1 - c0)]
                S.add("pool", lambda e, src=src, dstv=dstv: e.dma_start(out=dstv, in_=src),
                      reads=vs_reads(row0, row0 + nkbB * 128), writes=[("vg", 0), ("vg", 1)], dma=True)
            for hh in range(2):
                qsrc = QT[12 + hh * 4:12 + (hh + 1) * 4, :, Q0:Q0 + SQ].rearrange("h p t -> p h t")
                S.add("pool", lambda e, qsrc=qsrc: e.dma_start(out=qg, in_=qsrc),
                      reads=kv_reads(Q0, Q0 + SQ, list(range(12 + hh * 4, 16 + hh * 4)), "QT"), writes=["qg"], dma=True)
                for h4 in range(4):
                    hq = hh * 4 + h4
                    for n0 in range(0, SQ // 128, 2):
                        grp = []
                        for n in range(n0, n0 + 2):
                            qv = qg[:, h4, n * 128:(n + 1) * 128]
                            keys = [kg_[:, hh, (n + j) * 128:(n + j + 1) * 128] for j in range(3)]
                            grp.append(dict(q=qv, nq=128, keys=keys, vkb=[n, n + 1, n + 2], reads=["qg", "kg"]))
                        ov = ost[:, h4, n0 * 128:(n0 + 2) * 128].rearrange("p (n c) -> p n c", c=128)
                        grp[0]["outv"] = (hq, ov, "ost")
                        run_batch(grp, 3, 24 + hq * 3, hh * 128, "B")
                flush()
                od = OT[4 + hh * 4:8 + hh * 4, :, Q0:Q0 + SQ].rearrange("h p t -> p h t")
                S.add("pool", lambda e, od=od: e.dma_start(out=od, in_=ost), reads=["ost"],
                      writes=[("OT", 4 + hh * 4 + c, tt) for c in range(4) for tt in range(Q0 // T, (Q0 + SQ) // T)], dma=True)

        if L == 0:
            pump_conv(0, 10 ** 6)
        S.barrier()
        c2 = Carver(PBASE)
        xt = c2.take([128, 4, D], F32)
        oT = c2.take([128, 16, T], BF16)
        gring = c2.take([128, 4, 2, T], BF16)
        mT = c2.take([128, 16, T], BF16)
        hT = c2.take([128, 44, T], BF16)
        tmp = [c2.take([128, T], F32) for _ in range(2)]
        tmpb = [c2.take([128, T], F32) for _ in range(2)]
        lnt = c2.take([128, 2, D], F32)
        NW2 = 6
        wr2 = WRing(c2.take([128, NW2, UNIT], BF16), NW2)
        x1bf = mT.rearrange("p a b -> p (a b)").rearrange("p (s k) -> p s k", s=4)
        stg = hT.rearrange("p a b -> p (a b)")[:, 0:16384].bitcast(F32).rearrange("p (s k) -> p s k", s=4)
        gi = [0]
        ti_ = [0]
        U2 = NU1

        for t in full_tiles:
            load_x_tile(L, t, xt, "xt_all")
            osrc = OT[:, :, t * T:(t + 1) * T].rearrange("c p t -> p c t")
            S.add("sp", lambda e, osrc=osrc: e.dma_start(out=oT[:, 0:12, :], in_=osrc),
                  reads=[("OT", c, t) for c in range(12)], writes=["oT"] + [("x1T", s_, g_) for s_ in range(4) for g_ in range(4)], dma=True)
            for un in range(8):
                slot = wr2.load(L, U2 + un, 3072)
                for j in range(2):
                    oc = un * 2 + j
                    gs = gi[0] % 4
                    gi[0] += 1
                    gsrc = GT[:, :, t * T:(t + 1) * T].rearrange("(a c) p t -> c p a t", a=2)[oc]
                    S.add("sp", lambda e, gsrc=gsrc, gs=gs: e.dma_start(out=gring[:, gs, :, :], in_=gsrc),
                          reads=[("GT", oc, t), ("GT", 16 + oc, t)], writes=[("gr", gs)], dma=True)
                    ba, bb = (0, 2) if oc % 2 == 0 else (1, 3)

                    def mm(e, slot=slot, j=j, ba=ba, bb=bb):
                        ins = None
                        wv = wr2.view[:, slot, j * 1536:(j + 1) * 1536].rearrange("p (kc c) -> p kc c", c=128)
                        for kc in range(4):
                            ins = e.matmul(psum[:, ba, :], wv[:, kc, :], oT[:, kc, :], start=(kc == 0), stop=(kc == 3))
                        for kc in range(8):
                            ins = e.matmul(psum[:, bb, :], wv[:, 4 + kc, :], oT[:, 4 + kc, :], start=(kc == 0), stop=(kc == 7))
                        return ins
                    S.add("pe", mm, reads=[("wr", slot), "oT"], writes=[("ps", ba), ("ps", bb)])
                    tb = ti_[0] % 2
                    ti_[0] += 1
                    S.add("dve", lambda e, gs=gs, ba=ba, tb=tb: e.tensor_tensor(tmp[tb], psum[:, ba, :], gring[:, gs, 0, :], ALU.mult),
                          reads=[("ps", ba), ("gr", gs)], writes=[("tmp", tb)])
                    S.add("dve", lambda e, gs=gs, bb=bb, tb=tb: e.tensor_tensor(tmpb[tb], psum[:, bb, :], gring[:, gs, 1, :], ALU.mult),
                          reads=[("ps", bb), ("gr", gs)], writes=[("tmpb", tb)])
                    S.add("dve", lambda e, tb=tb, oc=oc: e.tensor_tensor(mT[:, oc, :], tmp[tb], tmpb[tb], ALU.add),
                          reads=[("tmp", tb), ("tmpb", tb)], writes=[("mT", oc)], ss=True)
            mT_keys = [("mT", oc) for oc in range(16)]

            def a_block(u0, nkc, units, lhs, lhs_keys, resid, rkey):
                u = u0
                for cb in range(4):
                    kc0 = 0
                    for kg, nk in enumerate(units):
                        slot = wr2.load(L, u, nk * 512)
                        u += 1
                        for s in range(4):
                            def mm(e, slot=slot, nk=nk, kc0=kc0, s=s):
                                ins = None
                                wv = wr2.view[:, slot, 0:nk * 512].rearrange("p (kc c) -> p kc c", c=512)
                                for k8 in range(nk):
                                    kc = kc0 + k8
                                    ins = e.matmul(psum[:, 4 + s, :], lhs[:, kc, s * 128:(s + 1) * 128], wv[:, k8, :],
                                                   start=(kc == 0), stop=(kc == nkc - 1))
                                return ins
                            S.add("pe", mm, reads=[("wr", slot)] + lhs_keys, writes=[("ps", 4 + s)])
                        kc0 += nk
                    for s in range(4):
                        rv = resid[:, s, cb * 512:(cb + 1) * 512]
                        S.add("dve", lambda e, rv=rv, s=s: e.scalar_tensor_tensor(rv, rv, ALPHA, psum[:, 4 + s, :], ALU.mult, ALU.add),
                              reads=[("ps", 4 + s), (rkey, s)] + (["xt_all"] if rkey == "xt" else []), writes=[(rkey, s)])
                        ln_stats_chunk(resid, rkey, s, cb)
                return u

            a_block(U2 + 8, 16, [8, 8], mT, mT_keys, xt, "xt")
            S.add("pool", lambda e, L=L: e.dma_start(out=lnt[:, 0, :], in_=ln_d["ln1_g"][L:L + 1, :].to_broadcast([128, D])),
                  writes=["lnt"], dma=True)
            S.add("pool", lambda e, L=L: e.dma_start(out=lnt[:, 1, :], in_=ln_d["ln1_b"][L:L + 1, :].to_broadcast([128, D])),
                  writes=["lnt"], dma=True)
            layer_norm_all(xt, "xt", lnt[:, 0, :], lnt[:, 1, :], "lnt", x1bf, "x1bf")
            x1T = oT
            x1T_keys = transpose_tile(x1bf, [("x1bf", s) for s in range(4)], x1T, "x1T")
            for c in range(44):
                slot = wr2.load(L, U2 + 16 + c)
                bg, bu = (0, 2) if c % 2 == 0 else (1, 3)

                def mm(e, slot=slot, bg=bg, bu=bu):
                    ins = None
                    wv = wr2.view[:, slot, :].rearrange("p (j kc c) -> p j kc c", j=2, c=128)
                    for kc in range(16):
                        ins = e.matmul(psum[:, bg, :], wv[:, 0, kc, :], x1T[:, kc, :], start=(kc == 0), stop=(kc == 15))
                    for kc in range(16):
                        ins = e.matmul(psum[:, bu, :], wv[:, 1, kc, :], x1T[:, kc, :], start=(kc == 0), stop=(kc == 15))
                    return ins
                S.add("pe", mm, reads=[("wr", slot)] + x1T_keys, writes=[("ps", bg), ("ps", bu)])
                tb = ti_[0] % 2
                ti_[0] += 1
                S.add("act", lambda e, bg=bg, tb=tb: e.activation(tmp[tb], psum[:, bg, :], AF.Silu),
                      reads=[("ps", bg)], writes=[("tmp", tb)])
                S.add("dve", lambda e, bu=bu, tb=tb, c=c: e.tensor_tensor(hT[:, c, :], psum[:, bu, :], tmp[tb], ALU.mult),
                      reads=[("ps", bu), ("tmp", tb)], writes=[("hT", c)])
            hT_keys = [("hT", c) for c in range(44)]
            a_block(U2 + 60, 44, [8, 8, 8, 8, 8, 4], hT, hT_keys, xt, "xt")
            S.add("pool", lambda e, L=L: e.dma_start(out=lnt[:, 0, :], in_=ln_d["ln2_g"][L:L + 1, :].to_broadcast([128, D])),
                  writes=["lnt"], dma=True)
            S.add("pool", lambda e, L=L: e.dma_start(out=lnt[:, 1, :], in_=ln_d["ln2_b"][L:L + 1, :].to_broadcast([128, D])),
                  writes=["lnt"], dma=True)
            layer_norm_all(xt, "xt", lnt[:, 0, :], lnt[:, 1, :], "lnt", None, None, out_stage=stg, stage_keys=hT_keys)
            if L == 0:
                dv = X1[t * T:(t + 1) * T, :].rearrange("(s p) k -> p s k", p=128)
                wk = [("X1", t)]
            else:
                dv = out[(t - 4) * T:(t - 3) * T, :].rearrange("(s p) k -> p s k", p=128)
                wk = [("out", t)]
            S.add("pool", lambda e, dv=dv: e.dma_start(out=dv, in_=stg), reads=[("stg", s) for s in range(4)] + hT_keys,
                  writes=wk, dma=True)

    S.barrier()
    S.finalize()
    return nc


_CACHE = {}


def kernel(x, rel_bias, w_in, sink, w_pa, w_pb, w_out, ln1_g, ln1_b, w_up, w_down, ln2_g, ln2_b, _debug=None):
    x = np.asarray(x, np.float32)
    if _debug is not None:
        nc = build_program(debug=True, nlayers=_debug)
    else:
        if "nc" not in _CACHE:
            _CACHE["nc"] = build_program()
        nc = _CACHE["nc"]
    bias_all = bias_tiles(np.asarray(rel_bias, np.float32)).reshape(128, 48 * 128)
    ident = np.eye(128, dtype=np.float32).astype(ml_dtypes.bfloat16)
    shared = {
        "bias_all": np.ascontiguousarray(bias_all),
        "ident": ident,
        "sink": np.ascontiguousarray(np.asarray(sink, np.float32).reshape(1, 16)),
        "ln1_g": np.asarray(ln1_g, np.float32), "ln1_b": np.asarray(ln1_b, np.float32),
        "ln2_g": np.asarray(ln2_g, np.float32), "ln2_b": np.asarray(ln2_b, np.float32),
        "w_in": np.asarray(w_in, np.float32), "w_pa": np.asarray(w_pa, np.float32),
        "w_pb": np.asarray(w_pb, np.float32), "w_out": np.asarray(w_out, np.float32),
        "w_up": np.asarray(w_up, np.float32), "w_down": np.asarray(w_down, np.float32),
    }
    in_maps = []
    for c in range(8):
        b, qi = c // 4, c % 4
        lo = qi * 4096 - 2048
        xe = np.zeros((EXT, D), np.float32)
        a, z = max(lo, 0), min(lo + EXT, SEQ)
        xe[a - lo:z - lo] = x[b, a:z]
        valid = np.zeros((EXT,), np.float32)
        valid[a - lo:z - lo] = 1.0
        m = dict(shared)
        m["x_ext"] = xe
        m["validrep"] = np.ascontiguousarray(np.repeat(valid[:, None], 128, axis=1).astype(ml_dtypes.bfloat16))
        m["validcol"] = np.ascontiguousarray(valid.reshape(EXT // 128, 128).T)
        in_maps.append(m)
    res = run_bass_kernel_spmd(nc, in_maps, core_ids=list(range(8)))
    if _debug is not None:
        return res.results, in_maps
    outp = np.empty((2, SEQ, D), np.float32)
    for c in range(8):
        b, qi = c // 4, c % 4
        outp[b, qi * 4096:(qi + 1) * 4096] = res.results[c]["out"]
    return outp
```

```python
import contextlib
import math
import numpy as np
import ml_dtypes
import concourse.bass as bass
import concourse.mybir as mybir
from concourse.bass_utils import run_bass_kernel_spmd

F32 = mybir.dt.float32
BF16 = mybir.dt.bfloat16
AF = mybir.ActivationFunctionType
ALU = mybir.AluOpType

D = 2048
DFF = 5632
CIN = 10240
SEQ = 16384
EXT = 8192
T = 512
NT = EXT // T
VW = 1920
SCALE = 128.0 ** -0.5
ALPHA = 4.0 ** 0.25
LN_EPS = 1e-5
NEG = -30000.0
UNIT = 4096
NU1 = 41
NU2 = 84
NUNIT = NU1 + NU2
SQ = 2048


class Op:
    __slots__ = ("eng", "emit", "reads", "writes", "dma", "waits", "ms", "dsem", "dval")

    def __init__(self, eng, emit, reads, writes, dma):
        self.eng = eng
        self.emit = emit
        self.reads = reads
        self.writes = writes
        self.dma = dma
        self.waits = []
        self.ms = None
        self.dsem = None
        self.dval = None


class Sched:
    ENGS = ("pe", "act", "dve", "pool", "sp")
    SELF_SYNC = ("pool",)
    NDSEM = {"sp": 12, "pool": 24, "act": 4}

    def __init__(self, nc):
        self.nc = nc
        self.ops = {e: [] for e in self.ENGS}
        self.all = []
        self.lastw = {}
        self.readers = {}
        self.dcount = {}
        self.drr = {e: 0 for e in self.ENGS}
        self.dlast = {}

    def add(self, eng, emit, reads=(), writes=(), dma=False, ss=False):
        op = Op(eng, emit, tuple(reads), tuple(writes), dma)
        deps = []
        for r in op.reads:
            w = self.lastw.get(r)
            if w is not None:
                deps.append(w)
        for r in op.writes:
            w = self.lastw.get(r)
            if w is not None:
                deps.append(w)
            deps.extend(self.readers.get(r, ()))
        if dma:
            n = self.NDSEM[eng]
            slot = self.drr[eng] % n
            self.drr[eng] += 1
            key = (eng, slot)
            prev = self.dlast.get(key)
            if prev is not None:
                deps.append(prev)
            c = self.dcount.get(key, 0) + 1
            self.dcount[key] = c
            op.dsem = key
            op.dval = 16 * c
            self.dlast[key] = op
        seen = set()
        for d in deps:
            if id(d) in seen or d is op:
                continue
            seen.add(id(d))
            if d.dma:
                op.waits.append(("d", d.dsem, d.dval, d))
            else:
                if d.eng == eng and eng not in self.SELF_SYNC and not ss:
                    continue
                op.waits.append(("e", d.eng, None, d))
        for r in op.reads:
            self.readers.setdefault(r, []).append(op)
        for r in op.writes:
            self.lastw[r] = op
            self.readers[r] = []
        self.ops[eng].append(op)
        self.all.append(op)
        return op

    def barrier(self):
        lasts = []
        for e in self.ENGS:
            for op in reversed(self.ops[e]):
                if (not op.dma) and op.emit is not None:
                    lasts.append(op)
                    break
        dlasts = list(self.dlast.values())
        for e in self.ENGS:
            op = Op(e, None, (), (), False)
            for d in lasts:
                if d.eng != e:
                    op.waits.append(("e", d.eng, None, d))
            for d in dlasts:
                op.waits.append(("d", d.dsem, d.dval, d))
            self.ops[e].append(op)
            self.all.append(op)
        self.lastw = {}
        self.readers = {}

    def finalize(self):
        nc = self.nc
        for op in self.all:
            for w in op.waits:
                if w[0] == "e":
                    w[3].ms = -1
        cnt = {e: 0 for e in self.ENGS}
        for e in self.ENGS:
            for op in self.ops[e]:
                if op.ms == -1:
                    cnt[e] += 1
                    op.ms = cnt[e]
        st = contextlib.ExitStack()
        esem = {e: st.enter_context(nc.semaphore("s_" + e)) for e in self.ENGS}
        dsem = {}
        for key in self.dcount:
            dsem[key] = st.enter_context(nc.semaphore("d_%s%d" % key))
        block = st.enter_context(nc.Block())
        engobj = {"pe": "tensor", "act": "scalar", "dve": "vector", "pool": "gpsimd", "sp": "sync"}

        def run(e, eng):
            waited = {}
            for op in self.ops[e]:
                for w in op.waits:
                    if w[0] == "e":
                        sem = esem[w[1]]
                        val = w[3].ms
                        k = ("e", w[1])
                    else:
                        sem = dsem[w[1]]
                        val = w[2]
                        k = ("d", w[1])
                    if waited.get(k, 0) >= val:
                        continue
                    waited[k] = val
                    eng.wait_ge(sem, val)
                if op.emit is None:
                    continue
                ins = op.emit(eng)
                if op.dma:
                    ins.then_inc(dsem[op.dsem], 16)
                elif op.ms is not None:
                    ins.then_inc(esem[e], 1)

        for e in self.ENGS:
            if not self.ops[e]:
                continue
            getattr(block, engobj[e])(lambda eng, e=e: run(e, eng))
        st.close()


def t5_bucket(rel):
    half = 16
    max_exact = 8
    n = np.abs(rel)
    scaled = np.log(np.maximum(n, 1) / max_exact) / math.log(1024 / max_exact)
    large = np.minimum(max_exact + (scaled * (half - max_exact)).astype(np.int64), half - 1)
    return (np.where(rel > 0, half, 0) + np.where(n < max_exact, n, large)).astype(np.int32)


A_DIL = (1, 4, 16)


def bias_tiles(rel_bias):
    out = np.full((128, 48, 128), NEG, np.float32)
    p = np.arange(128)[:, None]
    f = np.arange(128)[None, :]
    for g in range(3):
        for h in range(4):
            head = g * 4 + h
            for j in range(2):
                rel = 128 * j - 64 + p - f
                vals = rel_bias[t5_bucket(rel * A_DIL[g]), head]
                out[:, head * 2 + j, :] = np.where(np.abs(rel) <= 64, vals, NEG)
    for h in range(8):
        for j in range(3):
            rel = 128 * j - 128 + p - f
            vals = rel_bias[t5_bucket(rel), 12 + h]
            out[:, 24 + h * 3 + j, :] = np.where(np.abs(rel) <= 128, vals, NEG)
    return out


K_CH = list(range(12, 24)) + [44, 45]
Q_CH = list(range(0, 12)) + list(range(36, 44))
G_CH = list(range(48, 80))


def unit_plan():
    units = []
    for i in range(0, 14, 2):
        units.append(("B", "w_in", 16, 0, [K_CH[i], K_CH[i + 1]]))
    for cb in range(4):
        c0 = 3072 + cb * 512 if cb < 3 else 5888
        nc_ = 512 if cb < 3 else 256
        for kg in range(2):
            units.append(("A", "w_in", kg * 8, 8, c0, nc_))
    for i in range(0, 20, 2):
        units.append(("B", "w_in", 16, 0, [Q_CH[i], Q_CH[i + 1]]))
    for i in range(0, 32, 2):
        units.append(("B", "w_in", 16, 0, [G_CH[i], G_CH[i + 1]]))
    assert len(units) == NU1
    for oc in range(0, 16, 2):
        units.append(("PAB", oc))
    for cb in range(4):
        for kg in range(2):
            units.append(("A", "w_out", kg * 8, 8, cb * 512, 512))
    for c in range(44):
        units.append(("B", "w_up", 16, 0, [c, 44 + c]))
    for cb in range(4):
        for kg in range(6):
            nk = 8 if kg < 5 else 4
            units.append(("A", "w_down", kg * 8, nk, cb * 512, 512))
    assert len(units) == NUNIT
    return units


UNITS = unit_plan()


def build_program(debug=False, nlayers=2):
    nc = bass.Bass("TRN2", target_bir_lowering=False)

    def din(name, shape, dt=F32):
        return nc.dram_tensor(name, list(shape), dt, kind="ExternalInput").ap()

    def dscr(name, shape, dt):
        kind = "ExternalOutput" if (debug and name != "WB") else "Internal"
        return nc.dram_tensor(name, list(shape), dt, kind=kind).ap()

    x_ext = din("x_ext", [EXT, D])
    validrep = din("validrep", [EXT, 128], BF16)
    validcol_d = din("validcol", [128, EXT // 128])
    bias_d = din("bias_all", [128, 48 * 128])
    ident_d = din("ident", [128, 128], BF16)
    sink_d = din("sink", [1, 16])
    ln_d = {n: din(n, [2, D]) for n in ("ln1_g", "ln1_b", "ln2_g", "ln2_b")}
    W = {
        "w_in": din("w_in", [2, D, CIN]),
        "w_pa": din("w_pa", [2, 512, D]),
        "w_pb": din("w_pb", [2, 1024, D]),
        "w_out": din("w_out", [2, D, D]),
        "w_up": din("w_up", [2, D, 2 * DFF]),
        "w_down": din("w_down", [2, DFF, D]),
    }
    out = nc.dram_tensor("out", [4096, D], F32, kind="ExternalOutput").ap()

    WB = dscr("WB", [2, NUNIT, 128, UNIT], BF16)
    QT = dscr("QT", [20, 128, EXT], BF16)
    KT = dscr("KT", [14, 128, EXT], BF16)
    VS = dscr("VS", [EXT, VW], BF16)
    GT = dscr("GT", [32, 128, EXT], BF16)
    OT = dscr("OT", [12, 128, EXT], BF16)
    X1 = dscr("X1", [EXT, D], F32)

    S = Sched(nc)
    ARENA_BYTES = 204 * 1024
    arena = nc.alloc_sbuf_tensor("arena", [128, ARENA_BYTES // 2], BF16)[:]
    psum = nc.alloc_psum_tensor("psum", [128, 8, 512], F32)[:]
    psum_bf = psum.bitcast(BF16)

    class Carver:
        def __init__(self, base):
            self.off = base

        def take(self, shape, dt):
            n = 1
            for s_ in shape[1:]:
                n *= s_
            nb = n * (4 if dt == F32 else 2)
            nb_al = (nb + 63) // 64 * 64
            v = arena[:, self.off // 2:(self.off + nb) // 2]
            if dt == F32:
                v = v.bitcast(F32)
            self.off += nb_al
            assert self.off <= ARENA_BYTES, self.off
            if len(shape) == 3:
                v = v.rearrange("p (a b) -> p a b", a=shape[1])
            elif len(shape) == 4:
                v = v.rearrange("p (a b c) -> p a b c", a=shape[1], b=shape[2])
            return v

    pc = Carver(0)
    ident = pc.take([128, 128], BF16)
    validcol = pc.take([128, EXT // 128], F32)
    esink = pc.take([128, 16], F32)
    stat = pc.take([128, 160], F32)
    PBASE = pc.off

    S.add("pool", lambda e: e.dma_start(out=ident, in_=ident_d), writes=["ident"], dma=True)
    S.add("pool", lambda e: e.dma_start(out=validcol, in_=validcol_d), writes=["validcol"], dma=True)
    S.add("pool", lambda e: e.dma_start(out=esink, in_=sink_d.to_broadcast([128, 16])), writes=["esink"], dma=True)
    S.add("act", lambda e: e.activation(esink, esink, AF.Exp), reads=["esink"], writes=["esink"])
    for i in range(8):
        S.add("pool", lambda e, i=i: e.dma_start(out=VS[i * 1024:(i + 1) * 1024, 1792:1920],
                                                 in_=validrep[i * 1024:(i + 1) * 1024, :]),
              writes=[("VSvalid", i)], dma=True)

    def conv_ops(L):
        ops = []
        for u, un in enumerate(UNITS):
            dst = WB[L, u]
            if un[0] == "B":
                _, wn, nk, _, chs = un
                for j, c in enumerate(chs):
                    src = W[wn][L][:, c * 128:(c + 1) * 128].rearrange("(kc p) j -> p kc j", p=128)
                    d = dst[:, j * nk * 128:(j + 1) * nk * 128].rearrange("p (kc j) -> p kc j", j=128)
                    ops.append((u, d, src))
            elif un[0] == "A":
                _, wn, k0, nk, c0, ncol = un
                src = W[wn][L][k0 * 128:(k0 + nk) * 128, c0:c0 + ncol].rearrange("(kc p) j -> p kc j", p=128)
                d = dst[:, 0:nk * ncol].rearrange("p (kc j) -> p kc j", j=ncol)
                ops.append((u, d, src))
            else:
                oc0 = un[1]
                for j in range(2):
                    oc = oc0 + j
                    base = j * 1536
                    src = W["w_pa"][L][:, oc * 128:(oc + 1) * 128].rearrange("(kc p) j -> p kc j", p=128)
                    d = dst[:, base:base + 512].rearrange("p (kc j) -> p kc j", j=128)
                    ops.append((u, d, src))
                    src = W["w_pb"][L][:, oc * 128:(oc + 1) * 128].rearrange("(kc p) j -> p kc j", p=128)
                    d = dst[:, base + 512:base + 1536].rearrange("p (kc j) -> p kc j", j=128)
                    ops.append((u, d, src))
        return ops

    conv_pending = {0: conv_ops(0), 1: conv_ops(1)}
    conv_done_parts = {}

    def pump_conv(L, n):
        lst = conv_pending[L]
        for _ in range(min(n, len(lst))):
            u, d, src = lst.pop(0)
            k = conv_done_parts.get((L, u), 0)
            conv_done_parts[(L, u)] = k + 1
            S.add("pool", lambda e, d=d, src=src: e.dma_start(out=d, in_=src),
                  writes=[("wb", L, u, k)], dma=True)

    def wb_res(L, u):
        return [("wb", L, u, k) for k in range(conv_done_parts.get((L, u), 0))]

    pump_conv(0, 74)

    rr = {"ev": 0}

    def evac(dst, src, reads, writes, func=None, eng=None):
        if eng is None:
            if func is not None:
                eng = "act"
            else:
                eng = ("dve", "act")[rr["ev"] % 2]
                rr["ev"] += 1
        if eng == "act":
            f = func if func is not None else AF.Copy
            S.add("act", lambda e: e.activation(dst, src, f), reads=reads, writes=writes)
        else:
            S.add("dve", lambda e: e.tensor_copy(dst, src), reads=reads, writes=writes)

    class WRing:
        def __init__(self, view, nslots):
            self.view = view
            self.n = nslots
            self.i = 0

        def load(self, L, u, nelem=UNIT):
            slot = self.i % self.n
            self.i += 1
            if L == 0 and u >= NU1 and self.i % 4 == 0:
                pump_conv(1, 1)
            dstv = self.view[:, slot, 0:nelem]
            srcv = WB[L, u][:, 0:nelem]
            S.add("sp", lambda e: e.dma_start(out=dstv, in_=srcv),
                  reads=wb_res(L, u), writes=[("wr", slot)], dma=True)
            return slot

    def load_x_tile(L, t, xin, key):
        src = (x_ext if L == 0 else X1)[t * T:(t + 1) * T, :].rearrange("(s p) k -> p s k", p=128)
        rd = [] if L == 0 else [("X1", t)]
        wk_ = [key] + ([("xt", s_) for s_ in range(4)] if key == "xt_all" else [])
        S.add("pool", lambda e: e.dma_start(out=xin, in_=src), reads=rd, writes=wk_, dma=True)

    def transpose_tile(src_bf, src_keys, dstT, dst_key, banks=(0, 1)):
        n = 0
        for s in range(4):
            for g in range(4):
                bank = banks[n % len(banks)]
                n += 1

                def tr(e, s=s, g=g, bank=bank):
                    ins = None
                    for j in range(4):
                        kc = g * 4 + j
                        ins = e.transpose(psum_bf[:, bank, j * 128:(j + 1) * 128],
                                          src_bf[:, s, kc * 128:(kc + 1) * 128], ident)
                    return ins
                S.add("pe", tr, reads=[src_keys[s], "ident"], writes=[("ps", bank)])
                srcv = psum_bf[:, bank, 0:512].rearrange("p (j t) -> p j t", j=4)
                dstv = dstT[:, g * 4:(g + 1) * 4, s * 128:(s + 1) * 128]
                evac(dstv, srcv, [("ps", bank)], [(dst_key, s, g)])
        return [(dst_key, s, g) for s in range(4) for g in range(4)]

    st6a = stat[:, 0:96].rearrange("p (s c) -> p s c", s=4)
    mva = stat[:, 96:104].rearrange("p (s c) -> p s c", s=4)
    rstda = stat[:, 104:108]

    def ln_stats_chunk(xt, xkey, s, cb):
        xs = xt[:, s, cb * 512:(cb + 1) * 512]
        S.add("dve", lambda e: e.bn_stats(st6a[:, s, cb * 6:(cb + 1) * 6], xs), reads=[(xkey, s)],
              writes=[("st6", s, cb)], ss=True)

    def layer_norm_all(xt, xkey, lng, lnb, lnkey, xbf_out, xbf_key, out_stage=None, stage_keys=()):
        for s in range(4):
            S.add("dve", lambda e, s=s: e.bn_aggr(mva[:, s, :], st6a[:, s, :]),
                  reads=[("st6", s, cb) for cb in range(4)], writes=[("mv", s)], ss=True)
        S.add("dve", lambda e: e.tensor_scalar(rstda, mva[:, :, 1], LN_EPS, None, ALU.add),
              reads=[("mv", s) for s in range(4)], writes=["rstd"], ss=True)
        S.add("act", lambda e: e.sqrt(rstda, rstda), reads=["rstd"], writes=["rstd"])
        S.add("dve", lambda e: e.reciprocal(rstda, rstda), reads=["rstd"], writes=["rstd"], ss=True)
        for s in range(4):
            xs = xt[:, s, :]
            S.add("dve", lambda e, xs=xs, s=s: e.tensor_scalar(xs, xs, mva[:, s, 0:1], rstda[:, s:s + 1], ALU.subtract, ALU.mult),
                  reads=[(xkey, s), "rstd", ("mv", s)], writes=[(xkey, s)], ss=True)
            S.add("dve", lambda e, xs=xs: e.tensor_tensor(xs, xs, lng, ALU.mult), reads=[(xkey, s), lnkey], writes=[(xkey, s)])
            if out_stage is None:
                S.add("pool", lambda e, xs=xs: e.tensor_tensor(xs, xs, lnb, ALU.add), reads=[(xkey, s), lnkey], writes=[(xkey, s)])
            else:
                ov = out_stage[:, s, :]
                S.add("pool", lambda e, xs=xs, ov=ov: e.tensor_tensor(ov, xs, lnb, ALU.add), reads=[(xkey, s), lnkey],
                      writes=[("stg", s)] + list(stage_keys))
            if xbf_out is not None:
                S.add("act", lambda e, xs=xs, s=s: e.activation(xbf_out[:, s, :], xs, AF.Copy), reads=[(xkey, s)],
                      writes=[(xbf_key, s)] + [("mT", oc) for oc in range(16)])

    for L in range(nlayers):
        if L == 1:
            pump_conv(1, 10 ** 6)
        kv_tiles = list(range(0, 16)) if L == 0 else list(range(2, 14))
        full_tiles = list(range(2, 14)) if L == 0 else list(range(4, 12))

        S.barrier()
        c1 = Carver(PBASE)
        xin = [c1.take([128, 4, D], F32) for _ in range(2)]
        xbf = c1.take([128, 4, D], BF16)
        xT = c1.take([128, 16, T], BF16)
        NW1 = 8
        wr1 = WRing(c1.take([128, NW1, UNIT], BF16), NW1)
        stB = c1.take([128, 4, 2, T], BF16)
        vst = c1.take([128, 4, 1792], BF16)
        stb_i = [0]

        load_x_tile(L, kv_tiles[0], xin[0], ("xin", 0))
        for ti, t in enumerate(kv_tiles):
            full = t in full_tiles
            xi = xin[ti % 2]
            xkey = ("xin", ti % 2)
            if ti + 1 < len(kv_tiles):
                load_x_tile(L, kv_tiles[ti + 1], xin[(ti + 1) % 2], ("xin", (ti + 1) % 2))
            for s in range(4):
                evac(xbf[:, s, :], xi[:, s, :], [xkey], [("xbf", s)])
            xT_keys = transpose_tile(xbf, [("xbf", s) for s in range(4)], xT, "xT")

            def b_units(u0, nun, dests, func):
                for k in range(nun):
                    slot = wr1.load(L, u0 + k)
                    sb = stb_i[0] % 4
                    stb_i[0] += 1
                    for j in range(2):
                        bank = 2 + j

                        def mm(e, slot=slot, j=j, bank=bank):
                            ins = None
                            wv = wr1.view[:, slot, j * 2048:(j + 1) * 2048].rearrange("p (kc c) -> p kc c", c=128)
                            for kc in range(16):
                                ins = e.matmul(psum[:, bank, :], wv[:, kc, :], xT[:, kc, :],
                                               start=(kc == 0), stop=(kc == 15))
                            return ins
                        S.add("pe", mm, reads=[("wr", slot)] + xT_keys, writes=[("ps", bank)])
                        evac(stB[:, sb, j, :], psum[:, bank, :], [("ps", bank)], [("stB", sb, j)], func=func)
                    (dt0, i0), (dt1, i1) = dests[2 * k], dests[2 * k + 1]
                    if dt0 is dt1 and i1 == i0 + 1:
                        dv = dt0[1][i0:i0 + 2, :, t * T:(t + 1) * T].rearrange("c p t -> p c t")
                        S.add("pool", lambda e, dv=dv, sb=sb: e.dma_start(out=dv, in_=stB[:, sb, :, :]),
                              reads=[("stB", sb, 0), ("stB", sb, 1)], writes=[(dt0[0], i0, t), (dt0[0], i1, t)], dma=True)
                    else:
                        for j, (dtj, ij) in enumerate(((dt0, i0), (dt1, i1))):
                            dv = dtj[1][ij, :, t * T:(t + 1) * T]
                            S.add("pool", lambda e, dv=dv, sb=sb, j=j: e.dma_start(out=dv, in_=stB[:, sb, j, :]),
                                  reads=[("stB", sb, j)], writes=[(dtj[0], ij, t)], dma=True)

            KTd = ("KT", KT)
            QTd = ("QT", QT)
            GTd = ("GT", GT)
            b_units(0, 7, [(KTd, i) for i in range(14)], None)
            u = 7
            for cb in range(4):
                ncol = 512 if cb < 3 else 256
                c0 = cb * 512
                slots = [wr1.load(L, u + kg, 8 * ncol) for kg in range(2)]
                u += 2
                for kg in range(2):
                    for s in range(4):
                        def mm(e, slot=slots[kg], kg=kg, s=s, ncol=ncol):
                            ins = None
                            wv = wr1.view[:, slot, 0:8 * ncol].rearrange("p (kc c) -> p kc c", c=ncol)
                            for k8 in range(8):
                                kc = kg * 8 + k8
                                ins = e.matmul(psum[:, 4 + s, 0:ncol], xT[:, kc, s * 128:(s + 1) * 128], wv[:, k8, :],
                                               start=(kc == 0), stop=(kc == 15))
                            return ins
                        S.add("pe", mm, reads=[("wr", slots[kg])] + xT_keys, writes=[("ps", 4 + s)])
                for s in range(4):
                    blk = t * 4 + s
                    dstv = vst[:, s, c0:c0 + ncol]
                    srcv = psum[:, 4 + s, 0:ncol]
                    S.add("dve", lambda e, dstv=dstv, srcv=srcv, blk=blk: e.tensor_scalar(
                        dstv, srcv, validcol[:, blk:blk + 1], None, ALU.mult),
                        reads=[("ps", 4 + s), "validcol"], writes=[("vst", s, cb)])
            for s in range(4):
                dv = VS[t * T + s * 128:t * T + (s + 1) * 128, 0:1792]
                S.add("pool", lambda e, dv=dv, s=s: e.dma_start(out=dv, in_=vst[:, s, :]),
                      reads=[("vst", s, cb) for cb in range(4)], writes=[("VS", t * 4 + s)], dma=True)
            if full:
                b_units(15, 10, [(QTd, i) for i in range(20)], None)
                b_units(25, 16, [(GTd, i) for i in range(32)], AF.Sigmoid)

        S.barrier()
        ca = Carver(PBASE)
        biasT = ca.take([128, 48, 128], F32)
        acc = ca.take([128, 2, 4 * SQ], F32).rearrange("p a (h t) -> p a h t", h=4)
        accn = acc[:, 0]
        accd = acc[:, 1]
        qg = ca.take([128, 4, SQ], BF16)
        kg_ = ca.take([128, 4, SQ + 2048], BF16)
        vg = ca.take([128, 18, 640], BF16)
        tS = [ca.take([128, 1024], F32) for _ in range(2)]
        pT = [ca.take([128, 1024], BF16) for _ in range(2)]
        ost = ca.take([128, 4, SQ], BF16)
        rden = [ca.take([128, 512], F32) for _ in range(2)]
        S.add("pool", lambda e: e.dma_start(out=biasT.rearrange("p a b -> p (a b)"), in_=bias_d), writes=["biasT"], dma=True)
        bcnt = [0]
        pend = [None]
        vhalf = [0]

        def kv_reads(lo, hi, heads, kind):
            t0, t1 = lo // T, (hi - 1) // T
            return [(kind, h, tt) for h in heads for tt in range(t0, t1 + 1)]

        def vs_reads(lo, hi):
            return [("VS", b) for b in range(lo // 128, (hi - 1) // 128 + 1)] + [("VSvalid", i) for i in range(8)]

        def run_batch(qblocks, nj, bias_idx0, vcol0, kind):
            b = bcnt[0] % 2
            bcnt[0] += 1
            if L == 0 and bcnt[0] % 2 == 0:
                pump_conv(0, 1)
            nq = qblocks[0]["nq"]
            nb = len(qblocks)
            sb0 = 2 * b

            def qk(e):
                ins = None
                for i, qb in enumerate(qblocks):
                    for j in range(nj):
                        col = (i * nj + j) * 128
                        ins = e.matmul(psum[:, sb0 + col // 512, col % 512:col % 512 + nq], qb["keys"][j], qb["q"],
                                       start=True, stop=True)
                return ins
            S.add("pe", qk, reads=qblocks[0]["reads"], writes=[("ps", sb0), ("ps", sb0 + 1)])
            ntot = nb * nj
            pv_s = psum[:, sb0:sb0 + 2, :].rearrange("p a b -> p (a b)")[:, 0:ntot * 128].rearrange(
                "p (i j c) -> p i j c", i=nb, j=nj)[:, :, :, 0:nq]
            tv = tS[b][:, 0:ntot * 128].rearrange("p (i j c) -> p i j c", i=nb, j=nj)[:, :, :, 0:nq]
            bv = biasT[:, bias_idx0:bias_idx0 + nj, 0:nq].unsqueeze(1).to_broadcast([128, nb, nj, nq])
            S.add("dve", lambda e: e.scalar_tensor_tensor(tv, pv_s, SCALE, bv, ALU.mult, ALU.add),
                  reads=[("ps", sb0), ("ps", sb0 + 1), "biasT"], writes=[("tS", b)])
            pv = pT[b][:, 0:ntot * 128].rearrange("p (i j c) -> p i j c", i=nb, j=nj)[:, :, :, 0:nq]
            S.add("act", lambda e: e.activation(pv, tv, AF.Exp), reads=[("tS", b)], writes=[("pT", b)])

            def pvm(e):
                ins = None
                for i, qb in enumerate(qblocks):
                    for j in range(nj):
                        col = (i * nj + j) * 128
                        ins = e.matmul(psum[:, 4 + b, i * 128:i * 128 + nq],
                                       vg[:, qb["vkb"][j], vcol0:vcol0 + 128], pT[b][:, col:col + nq],
                                       start=(j == 0), stop=(j == nj - 1))
                for i, qb in enumerate(qblocks):
                    for j in range(nj):
                        col = (i * nj + j) * 128
                        ins = e.matmul(psum[:, 6 + b, i * 128:i * 128 + nq],
                                       vg[:, qb["vkb"][j], 512:640], pT[b][:, col:col + nq],
                                       start=(j == 0), stop=(j == nj - 1))
                return ins
            po = psum[:, 4 + b, :].rearrange("p (i c) -> p i c", c=128)[:, 0:nb, 0:nq]
            pd = psum[:, 6 + b, :].rearrange("p (i c) -> p i c", c=128)[:, 0:nb, 0:nq]
            Lc = L

            def stageB():
                S.add("pe", pvm, reads=[("pT", b)] + qblocks[0].get("vkeys", [("vg", 0), ("vg", 1)]), writes=[("ps", 4 + b), ("ps", 6 + b)])
                if kind == "A":
                    av, akey = qblocks[0]["accv"]
                    pod = psum[:, 4 + b:8:2, :].rearrange("p a (i c) -> p a i c", c=128)[:, :, 0:nb, 0:nq]
                    S.add("dve", lambda e: e.tensor_tensor(av, av, pod, ALU.add),
                          reads=[("ps", 4 + b), ("ps", 6 + b), akey], writes=[akey])
                else:
                    hq, ov, okey = qblocks[0]["outv"]
                    rv = rden[b][:, 0:nb * 128].rearrange("p (i c) -> p i c", c=128)[:, :, 0:nq]
                    S.add("act", lambda e: e.activation(rv, pd, AF.Ln, bias=esink[:, Lc * 8 + hq:Lc * 8 + hq + 1], scale=1.0),
                          reads=[("ps", 6 + b), "esink"], writes=[("rden", b)])
                    S.add("act", lambda e: e.activation(rv, rv, AF.Exp, scale=-1.0), reads=[("rden", b)], writes=[("rden", b)], ss=True)
                    S.add("dve", lambda e: e.tensor_tensor(ov, po, rv, ALU.mult), reads=[("ps", 4 + b), ("rden", b)], writes=[okey])
            prev = pend[0]
            pend[0] = stageB
            if prev is not None:
                prev()

        def flush():
            f = pend[0]
            pend[0] = None
            if f is not None:
                f()

        q_lo = full_tiles[0] * T
        q_hi = (full_tiles[-1] + 1) * T
        for Q0 in range(q_lo, q_hi, SQ):
            S.add("pool", lambda e: e.memset(accn, 0.0), writes=["acc"])
            S.add("pool", lambda e: e.memset(accd, 1e-30), writes=["acc"])
            for g in range(3):
                dil = A_DIL[g]
                Wd = 64 * dil
                heads = [g * 4 + h for h in range(4)]
                flush()
                klen = SQ + 2 * Wd
                for h_ in range(4):
                    qsrc = QT[g * 4 + h_, :, Q0:Q0 + SQ]
                    S.add("pool", lambda e, qsrc=qsrc, h_=h_: e.dma_start(out=qg[:, h_, :], in_=qsrc),
                          reads=kv_reads(Q0, Q0 + SQ, [g * 4 + h_], "QT"), writes=[("qg", h_)], dma=True)
                    ksrc = KT[g * 4 + h_, :, Q0 - Wd:Q0 + SQ + Wd]
                    S.add("pool", lambda e, ksrc=ksrc, klen=klen, h_=h_: e.dma_start(out=kg_[:, h_, 0:klen], in_=ksrc),
                          reads=kv_reads(Q0 - Wd, Q0 + SQ + Wd, [g * 4 + h_], "KT"), writes=[("kg", h_)], dma=True)
                nsub = SQ // dil
                nq = min(128, nsub)
                nqb = nsub // nq
                nkb = nqb + 1
                for r0 in range(0, dil, max(1, 4 // nqb) if nqb < 4 else 1):
                    rs = list(range(r0, min(dil, r0 + (max(1, 4 // nqb) if nqb < 4 else 1))))
                    if dil == 1:
                        flush()
                        vbase = 0
                        vkeys = [("vg", 0), ("vg", 1)]
                    else:
                        vhalf[0] ^= 1
                        vbase = 9 * vhalf[0]
                        vkeys = [("vg", vhalf[0])]
                    for ri, r in enumerate(rs):
                        for part, (c0, c1, d0) in enumerate(((g * 512, g * 512 + 512, 0), (1792, 1920, 512))):
                            row0 = Q0 + r - Wd
                            src = bass.AP(VS.tensor, row0 * VW + c0, [[dil * VW, 128], [128 * dil * VW, nkb], [1, c1 - c0]])
                            dstv = vg[:, vbase + ri * nkb:vbase + (ri + 1) * nkb, d0:d0 + (c1 - c0)]
                            S.add("pool", lambda e, src=src, dstv=dstv: e.dma_start(out=dstv, in_=src),
                                  reads=vs_reads(row0, row0 + nkb * 128 * dil), writes=vkeys, dma=True)
                    for h in range(4):
                        head = g * 4 + h
                        allqb = []
                        for ri, r in enumerate(rs):
                            for n in range(nqb):
                                i0 = n * nq
                                qv = qg[:, h, :].rearrange("p (i q) -> p i q", q=dil)[:, i0:i0 + nq, r]
                                keys = []
                                for j in range(2):
                                    ks = i0 + 128 * j
                                    kv_ = kg_[:, h, 0:klen].rearrange("p (i q) -> p i q", q=dil)[:, ks:ks + 128, r]
                                    keys.append(kv_)
                                an = accn[:, h, :].rearrange("p (i q) -> p i q", q=dil)
                                ad = accd[:, h, :].rearrange("p (i q) -> p i q", q=dil)
                                allqb.append(dict(q=qv, nq=nq, keys=keys, vkb=[vbase + ri * nkb + n, vbase + ri * nkb + n + 1],
                                                  r=r, i0=i0, reads=[("qg", h), ("kg", h)], vkeys=vkeys))
                        for b0 in range(0, len(allqb), 4):
                            grp = allqb[b0:b0 + 4]
                            a2 = acc[:, :, h, :].rearrange("p a (i q) -> p a i q", q=dil)
                            if len(rs) == 1:
                                r = rs[0]
                                i0 = grp[0]["i0"]
                                av = a2[:, :, i0:i0 + len(grp) * nq, r].rearrange("p a (n c) -> p a n c", c=nq)
                            else:
                                ra = grp[0]["r"]
                                av = a2[:, :, 0:nq, ra:ra + len(grp)].rearrange("p a c n -> p a n c")
                            grp[0]["accv"] = (av, "acc")
                            run_batch(grp, 2, head * 2, h * 128, "A")
            flush()
            S.add("act", lambda e: e.activation(accd, accd, AF.Ln), reads=["acc"], writes=["acc"])
            S.add("act", lambda e: e.activation(accd, accd, AF.Exp, scale=-1.0), reads=["acc"], writes=["acc"], ss=True)
            S.add("dve", lambda e: e.tensor_tensor(ost, accn, accd, ALU.mult), reads=["acc"], writes=["ost"])
            od = OT[0:4, :, Q0:Q0 + SQ].rearrange("h p t -> p h t")
            S.add("pool", lambda e, od=od: e.dma_start(out=od, in_=ost), reads=["ost"],
                  writes=[("OT", c, tt) for c in range(4) for tt in range(Q0 // T, (Q0 + SQ) // T)], dma=True)
            flush()
            for hh_ in range(2):
                ksrc = KT[12 + hh_, :, Q0 - 128:Q0 + SQ + 128]
                S.add("pool", lambda e, ksrc=ksrc, hh_=hh_: e.dma_start(out=kg_[:, hh_, 0:SQ + 256], in_=ksrc),
                      reads=kv_reads(Q0 - 128, Q0 + SQ + 128, [12 + hh_], "KT"), writes=[("kg", hh_)], dma=True)
            nkbB = SQ // 128 + 2
            flush()
            for part, (c0, c1, d0) in enumerate(((1536, 1792, 0), (1792, 1920, 512))):
                row0 = Q0 - 128
                src = VS[row0:row0 + nkbB * 128, c0:c1].rearrange("(kb p) c -> p kb c", p=128)
                dstv = vg[:, 0:nkbB, d0:d0 + (c1 - c0)]
                S.add("pool", lambda e, src=src, dstv=dstv: e.dma_start(out=dstv, in_=src),
                      reads=vs_reads(row0, row0 + nkbB * 128), writes=[("vg", 0), ("vg", 1)], dma=True)
            for hh in range(2):
                for h_ in range(4):
                    qsrc = QT[12 + hh * 4 + h_, :, Q0:Q0 + SQ]
                    S.add("pool", lambda e, qsrc=qsrc, h_=h_: e.dma_start(out=qg[:, h_, :], in_=qsrc),
                          reads=kv_reads(Q0, Q0 + SQ, [12 + hh * 4 + h_], "QT"), writes=[("qg", h_)], dma=True)
                for h4 in range(4):
                    hq = hh * 4 + h4
                    for n0 in range(0, SQ // 128, 2):
                        grp = []
                        for n in range(n0, n0 + 2):
                            qv = qg[:, h4, n * 128:(n + 1) * 128]
                            keys = [kg_[:, hh, (n + j) * 128:(n + j + 1) * 128] for j in range(3)]
                            grp.append(dict(q=qv, nq=128, keys=keys, vkb=[n, n + 1, n + 2], reads=[("qg", h4), ("kg", hh)]))
                        ov = ost[:, h4, n0 * 128:(n0 + 2) * 128].rearrange("p (n c) -> p n c", c=128)
                        grp[0]["outv"] = (hq, ov, "ost")
                        run_batch(grp, 3, 24 + hq * 3, hh * 128, "B")
                flush()
                od = OT[4 + hh * 4:8 + hh * 4, :, Q0:Q0 + SQ].rearrange("h p t -> p h t")
                S.add("pool", lambda e, od=od: e.dma_start(out=od, in_=ost), reads=["ost"],
                      writes=[("OT", 4 + hh * 4 + c, tt) for c in range(4) for tt in range(Q0 // T, (Q0 + SQ) // T)], dma=True)

        if L == 0:
            pump_conv(0, 10 ** 6)
        S.barrier()
        c2 = Carver(PBASE)
        xt = c2.take([128, 4, D], F32)
        oT = c2.take([128, 16, T], BF16)
        gring = c2.take([128, 4, 2, T], BF16)
        mT = c2.take([128, 16, T], BF16)
        hT = c2.take([128, 44, T], BF16)
        tmp = [c2.take([128, T], F32) for _ in range(2)]
        tmpb = [c2.take([128, T], F32) for _ in range(2)]
        lnt = c2.take([128, 2, D], F32)
        NW2 = 6
        wr2 = WRing(c2.take([128, NW2, UNIT], BF16), NW2)
        x1bf = mT.rearrange("p a b -> p (a b)").rearrange("p (s k) -> p s k", s=4)
        stg = hT.rearrange("p a b -> p (a b)")[:, 0:16384].bitcast(F32).rearrange("p (s k) -> p s k", s=4)
        gi = [0]
        ti_ = [0]
        U2 = NU1

        for t in full_tiles:
            load_x_tile(L, t, xt, "xt_all")
            osrc = OT[:, :, t * T:(t + 1) * T].rearrange("c p t -> p c t")
            S.add("sp", lambda e, osrc=osrc: e.dma_start(out=oT[:, 0:12, :], in_=osrc),
                  reads=[("OT", c, t) for c in range(12)], writes=["oT"] + [("x1T", s_, g_) for s_ in range(4) for g_ in range(4)], dma=True)
            for un in range(8):
                slot = wr2.load(L, U2 + un, 3072)
                for j in range(2):
                    oc = un * 2 + j
                    gs = gi[0] % 4
                    gi[0] += 1
                    gsrc = GT[:, :, t * T:(t + 1) * T].rearrange("(a c) p t -> c p a t", a=2)[oc]
                    S.add("sp", lambda e, gsrc=gsrc, gs=gs: e.dma_start(out=gring[:, gs, :, :], in_=gsrc),
                          reads=[("GT", oc, t), ("GT", 16 + oc, t)], writes=[("gr", gs)], dma=True)
                    ba, bb = (0, 2) if oc % 2 == 0 else (1, 3)

                    def mm(e, slot=slot, j=j, ba=ba, bb=bb):
                        ins = None
                        wv = wr2.view[:, slot, j * 1536:(j + 1) * 1536].rearrange("p (kc c) -> p kc c", c=128)
                        for kc in range(4):
                            ins = e.matmul(psum[:, ba, :], wv[:, kc, :], oT[:, kc, :], start=(kc == 0), stop=(kc == 3))
                        for kc in range(8):
                            ins = e.matmul(psum[:, bb, :], wv[:, 4 + kc, :], oT[:, 4 + kc, :], start=(kc == 0), stop=(kc == 7))
                        return ins
                    S.add("pe", mm, reads=[("wr", slot), "oT"], writes=[("ps", ba), ("ps", bb)])
                    tb = ti_[0] % 2
                    ti_[0] += 1
                    S.add("dve", lambda e, gs=gs, ba=ba, tb=tb: e.tensor_tensor(tmp[tb], psum[:, ba, :], gring[:, gs, 0, :], ALU.mult),
                          reads=[("ps", ba), ("gr", gs)], writes=[("tmp", tb)])
                    S.add("dve", lambda e, gs=gs, bb=bb, tb=tb: e.tensor_tensor(tmpb[tb], psum[:, bb, :], gring[:, gs, 1, :], ALU.mult),
                          reads=[("ps", bb), ("gr", gs)], writes=[("tmpb", tb)])
                    S.add("dve", lambda e, tb=tb, oc=oc: e.tensor_tensor(mT[:, oc, :], tmp[tb], tmpb[tb], ALU.add),
                          reads=[("tmp", tb), ("tmpb", tb)], writes=[("mT", oc)], ss=True)
            mT_keys = [("mT", oc) for oc in range(16)]

            def a_block(u0, nkc, units, lhs, lhs_keys, resid, rkey):
                u = u0
                for cb in range(4):
                    kc0 = 0
                    for kg, nk in enumerate(units):
                        slot = wr2.load(L, u, nk * 512)
                        u += 1
                        for s in range(4):
                            def mm(e, slot=slot, nk=nk, kc0=kc0, s=s):
                                ins = None
                                wv = wr2.view[:, slot, 0:nk * 512].rearrange("p (kc c) -> p kc c", c=512)
                                for k8 in range(nk):
                                    kc = kc0 + k8
                                    ins = e.matmul(psum[:, 4 + s, :], lhs[:, kc, s * 128:(s + 1) * 128], wv[:, k8, :],
                                                   start=(kc == 0), stop=(kc == nkc - 1))
                                return ins
                            S.add("pe", mm, reads=[("wr", slot)] + lhs_keys, writes=[("ps", 4 + s)])
                        kc0 += nk
                    for s in range(4):
                        rv = resid[:, s, cb * 512:(cb + 1) * 512]
                        S.add("dve", lambda e, rv=rv, s=s: e.scalar_tensor_tensor(rv, rv, ALPHA, psum[:, 4 + s, :], ALU.mult, ALU.add),
                              reads=[("ps", 4 + s), (rkey, s)] + (["xt_all"] if rkey == "xt" else []), writes=[(rkey, s)])
                        ln_stats_chunk(resid, rkey, s, cb)
                return u

            a_block(U2 + 8, 16, [8, 8], mT, mT_keys, xt, "xt")
            S.add("pool", lambda e, L=L: e.dma_start(out=lnt[:, 0, :], in_=ln_d["ln1_g"][L:L + 1, :].to_broadcast([128, D])),
                  writes=["lnt"], dma=True)
            S.add("pool", lambda e, L=L: e.dma_start(out=lnt[:, 1, :], in_=ln_d["ln1_b"][L:L + 1, :].to_broadcast([128, D])),
                  writes=["lnt"], dma=True)
            layer_norm_all(xt, "xt", lnt[:, 0, :], lnt[:, 1, :], "lnt", x1bf, "x1bf")
            x1T = oT
            x1T_keys = transpose_tile(x1bf, [("x1bf", s) for s in range(4)], x1T, "x1T")
            for c in range(44):
                slot = wr2.load(L, U2 + 16 + c)
                bg, bu = (0, 2) if c % 2 == 0 else (1, 3)

                def mm(e, slot=slot, bg=bg, bu=bu):
                    ins = None
                    wv = wr2.view[:, slot, :].rearrange("p (j kc c) -> p j kc c", j=2, c=128)
                    for kc in range(16):
                        ins = e.matmul(psum[:, bg, :], wv[:, 0, kc, :], x1T[:, kc, :], start=(kc == 0), stop=(kc == 15))
                    for kc in range(16):
                        ins = e.matmul(psum[:, bu, :], wv[:, 1, kc, :], x1T[:, kc, :], start=(kc == 0), stop=(kc == 15))
                    return ins
                S.add("pe", mm, reads=[("wr", slot)] + x1T_keys, writes=[("ps", bg), ("ps", bu)])
                tb = ti_[0] % 2
                ti_[0] += 1
                S.add("act", lambda e, bg=bg, tb=tb: e.activation(tmp[tb], psum[:, bg, :], AF.Silu),
                      reads=[("ps", bg)], writes=[("tmp", tb)])
                S.add("dve", lambda e, bu=bu, tb=tb, c=c: e.tensor_tensor(hT[:, c, :], psum[:, bu, :], tmp[tb], ALU.mult),
                      reads=[("ps", bu), ("tmp", tb)], writes=[("hT", c)])
            hT_keys = [("hT", c) for c in range(44)]
            a_block(U2 + 60, 44, [8, 8, 8, 8, 8, 4], hT, hT_keys, xt, "xt")
            S.add("pool", lambda e, L=L: e.dma_start(out=lnt[:, 0, :], in_=ln_d["ln2_g"][L:L + 1, :].to_broadcast([128, D])),
                  writes=["lnt"], dma=True)
            S.add("pool", lambda e, L=L: e.dma_start(out=lnt[:, 1, :], in_=ln_d["ln2_b"][L:L + 1, :].to_broadcast([128, D])),
                  writes=["lnt"], dma=True)
            layer_norm_all(xt, "xt", lnt[:, 0, :], lnt[:, 1, :], "lnt", None, None, out_stage=stg, stage_keys=hT_keys)
            if L == 0:
                dv = X1[t * T:(t + 1) * T, :].rearrange("(s p) k -> p s k", p=128)
                wk = [("X1", t)]
            else:
                dv = out[(t - 4) * T:(t - 3) * T, :].rearrange("(s p) k -> p s k", p=128)
                wk = [("out", t)]
            S.add("pool", lambda e, dv=dv: e.dma_start(out=dv, in_=stg), reads=[("stg", s) for s in range(4)] + hT_keys,
                  writes=wk, dma=True)

    S.barrier()
    S.finalize()
    return nc


_CACHE = {}


def kernel(x, rel_bias, w_in, sink, w_pa, w_pb, w_out, ln1_g, ln1_b, w_up, w_down, ln2_g, ln2_b, _debug=None):
    x = np.asarray(x, np.float32)
    if _debug is not None:
        nc = build_program(debug=True, nlayers=_debug)
    else:
        if "nc" not in _CACHE:
            _CACHE["nc"] = build_program()
        nc = _CACHE["nc"]
    bias_all = bias_tiles(np.asarray(rel_bias, np.float32)).reshape(128, 48 * 128)
    ident = np.eye(128, dtype=np.float32).astype(ml_dtypes.bfloat16)
    shared = {
        "bias_all": np.ascontiguousarray(bias_all),
        "ident": ident,
        "sink": np.ascontiguousarray(np.asarray(sink, np.float32).reshape(1, 16)),
        "ln1_g": np.asarray(ln1_g, np.float32), "ln1_b": np.asarray(ln1_b, np.float32),
        "ln2_g": np.asarray(ln2_g, np.float32), "ln2_b": np.asarray(ln2_b, np.float32),
        "w_in": np.asarray(w_in, np.float32), "w_pa": np.asarray(w_pa, np.float32),
        "w_pb": np.asarray(w_pb, np.float32), "w_out": np.asarray(w_out, np.float32),
        "w_up": np.asarray(w_up, np.float32), "w_down": np.asarray(w_down, np.float32),
    }
    in_maps = []
    for c in range(8):
        b, qi = c // 4, c % 4
        lo = qi * 4096 - 2048
        xe = np.zeros((EXT, D), np.float32)
        a, z = max(lo, 0), min(lo + EXT, SEQ)
        xe[a - lo:z - lo] = x[b, a:z]
        valid = np.zeros((EXT,), np.float32)
        valid[a - lo:z - lo] = 1.0
        m = dict(shared)
        m["x_ext"] = xe
        m["validrep"] = np.ascontiguousarray(np.repeat(valid[:, None], 128, axis=1).astype(ml_dtypes.bfloat16))
        m["validcol"] = np.ascontiguousarray(valid.reshape(EXT // 128, 128).T)
        in_maps.append(m)
    res = run_bass_kernel_spmd(nc, in_maps, core_ids=list(range(8)))
    if _debug is not None:
        return res.results, in_maps
    outp = np.empty((2, SEQ, D), np.float32)
    for c in range(8):
        b, qi = c // 4, c % 4
        outp[b, qi * 4096:(qi + 1) * 4096] = res.results[c]["out"]
    return outp
```
